# Optimizing a Trainium2 kernel written in Bass

```python
import jax, jax.numpy as jnp
from jax import lax
import numpy as np

D_MODEL = 1024
BATCH = 8
SEQ = 4096
DEPTH = 4

N_HEADS = 8
HEAD_DIM = 64
W_ATTN = N_HEADS * HEAD_DIM
C_CONF = D_MODEL // 2
CONF_WIDTH = 31
C_SC = D_MODEL // 2
SC_WIDTH = 3
N_BRANCH = 3
D_FF = ((8 * D_MODEL + 2) // 3 + 255) // 256 * 256
BLOCK_Q = 128
EPS = 1e-6
SPLIT_SIZES = (W_ATTN, W_ATTN, W_ATTN, C_CONF, C_CONF, C_SC, C_SC, C_SC, N_BRANCH * D_MODEL)
SPLIT_IDX = tuple(int(i) for i in np.cumsum(SPLIT_SIZES)[:-1])
N_IN = int(sum(SPLIT_SIZES))

kernel_name = "hybrid_stickbreak_conformer_shortconv_gated"


def rms_norm(x, g):
    xf = x.astype(jnp.float32)
    y = xf * lax.rsqrt(jnp.mean(xf * xf, axis=-1, keepdims=True) + EPS)
    return (y * g.astype(jnp.float32)).astype(x.dtype)


def layer_norm(x, g, b):
    xf = x.astype(jnp.float32)
    mu = jnp.mean(xf, axis=-1, keepdims=True)
    var = jnp.mean(jnp.square(xf - mu), axis=-1, keepdims=True)
    y = (xf - mu) * lax.rsqrt(var + EPS)
    return (y * g.astype(jnp.float32) + b.astype(jnp.float32)).astype(x.dtype)


def causal_depthwise_conv(x, w):
    width, ch = w.shape
    return lax.conv_general_dilated(
        x, w[:, None, :].astype(x.dtype), window_strides=(1,), padding=[(width - 1, 0)],
        dimension_numbers=("NWC", "WIO", "NWC"), feature_group_count=ch)


def stick_breaking_attention(q, k, v):
    b, h, s, dh = q.shape
    nb = s // BLOCK_Q
    scale = float(1.0 / np.sqrt(dh))
    qb = q.reshape(b, h, nb, BLOCK_Q, dh).transpose(2, 0, 1, 3, 4)
    kpos = jnp.arange(s)

    def one_block(args):
        qi, bi = args
        z = jnp.einsum("bhqd,bhkd->bhqk", qi, k, preferred_element_type=jnp.float32) * scale
        qpos = bi * BLOCK_Q + jnp.arange(BLOCK_Q)
        causal = kpos[None, :] < qpos[:, None]
        log_1m_beta = jnp.where(causal, -jax.nn.softplus(z), 0.0)
        rest = lax.cumsum(log_1m_beta, axis=log_1m_beta.ndim - 1, reverse=True) - log_1m_beta
        attn = jnp.where(causal, jnp.exp(jax.nn.log_sigmoid(z) + rest), 0.0)
        return jnp.einsum("bhqk,bhkd->bhqd", attn.astype(v.dtype), v)

    out = lax.map(one_block, (qb, jnp.arange(nb)))
    return out.transpose(1, 2, 0, 3, 4).reshape(b, h, s, dh)


def setup_inputs(seed: int = 0) -> dict:
    key = jax.random.key(seed)
    ks = jax.random.split(key, 20)
    f32 = jnp.float32

    def nrm(k, shape, fan_in):
        return jax.random.normal(k, shape, f32) * (fan_in ** -0.5)

    def gain(k, shape):
        return 1.0 + 0.05 * jax.random.normal(k, shape, f32)

    def small(k, shape):
        return 0.02 * jax.random.normal(k, shape, f32)

    L = DEPTH
    return {
        "x": jax.random.normal(ks[0], (BATCH, SEQ, D_MODEL), f32),
        "mix_norm_g": gain(ks[1], (L, D_MODEL)),
        "w_in": nrm(ks[2], (L, D_MODEL, N_IN), D_MODEL),
        "q_norm_g": gain(ks[3], (L, HEAD_DIM)),
        "k_norm_g": gain(ks[4], (L, HEAD_DIM)),
        "w_attn_out": nrm(ks[5], (L, W_ATTN, D_MODEL), W_ATTN),
        "conf_dw_w": nrm(ks[6], (L, CONF_WIDTH, C_CONF), CONF_WIDTH),
        "conf_dw_b": small(ks[7], (L, C_CONF)),
        "conf_ln_g": gain(ks[8], (L, C_CONF)),
        "conf_ln_b": small(ks[9], (L, C_CONF)),
        "w_conf_out": nrm(ks[10], (L, C_CONF, D_MODEL), C_CONF),
        "sc_conv_w": nrm(ks[11], (L, SC_WIDTH, C_SC), SC_WIDTH),
        "w_sc_out": nrm(ks[12], (L, C_SC, D_MODEL), C_SC),
        "gate_b": small(ks[13], (L, N_BRANCH, D_MODEL)),
        "w_o": nrm(ks[14], (L, D_MODEL, D_MODEL), D_MODEL),
        "ffn_norm_g": gain(ks[15], (L, D_MODEL)),
        "w_ffn_in": nrm(ks[16], (L, D_MODEL, 2 * D_FF), D_MODEL),
        "w_ffn_out": nrm(ks[17], (L, D_FF, D_MODEL), D_FF),
    }


def reference(x, mix_norm_g, w_in, q_norm_g, k_norm_g, w_attn_out, conf_dw_w, conf_dw_b,
              conf_ln_g, conf_ln_b, w_conf_out, sc_conv_w, w_sc_out, gate_b, w_o,
              ffn_norm_g, w_ffn_in, w_ffn_out):
    b, s, _ = x.shape
    for l in range(DEPTH):
        u = rms_norm(x, mix_norm_g[l])
        proj = u @ w_in[l]
        q, k, v, cf_val, cf_gate, sc_x, sc_bg, sc_cg, gates = jnp.split(proj, SPLIT_IDX, axis=-1)

        q = rms_norm(q.reshape(b, s, N_HEADS, HEAD_DIM), q_norm_g[l]).transpose(0, 2, 1, 3)
        k = rms_norm(k.reshape(b, s, N_HEADS, HEAD_DIM), k_norm_g[l]).transpose(0, 2, 1, 3)
        v = v.reshape(b, s, N_HEADS, HEAD_DIM).transpose(0, 2, 1, 3)
        o = stick_breaking_attention(q, k, v).transpose(0, 2, 1, 3).reshape(b, s, W_ATTN)
        y_a = o @ w_attn_out[l]

        hb = cf_val * jax.nn.sigmoid(cf_gate)
        hb = causal_depthwise_conv(hb, conf_dw_w[l]) + conf_dw_b[l]
        hb = jax.nn.silu(layer_norm(hb, conf_ln_g[l], conf_ln_b[l]))
        y_b = hb @ w_conf_out[l]

        hc = sc_bg * causal_depthwise_conv(sc_cg * sc_x, sc_conv_w[l])
        y_c = hc @ w_sc_out[l]

        g = jax.nn.sigmoid(gates.reshape(b, s, N_BRANCH, D_MODEL) + gate_b[l])
        merged = g[:, :, 0] * y_a + g[:, :, 1] * y_b + g[:, :, 2] * y_c
        x = x + merged @ w_o[l]

        hf = rms_norm(x, ffn_norm_g[l]) @ w_ffn_in[l]
        gt, up = jnp.split(hf, 2, axis=-1)
        x = x + (jax.nn.silu(gt) * up) @ w_ffn_out[l]
    return x
```

```python
import contextlib
import numpy as np
import concourse.bass as bass
import concourse.mybir as mybir
from concourse.bass_utils import run_bass_kernel_spmd

F32 = mybir.dt.float32
BF16 = mybir.dt.bfloat16
AF = mybir.ActivationFunctionType
ALU = mybir.AluOpType

D = 1024
KD = 8
NH = 8
DFF = 2816
KF = 22
CW = 31
EPS = 1e-6
T = 512
NSLOT = 5
SLABW = 4096
NBLK = 1260
NSLAB = 40
NV = 190
V_MG, V_FG, V_QG, V_KG, V_CW, V_CB, V_LG, V_LB, V_SW, V_GB = 0, 8, 16, 17, 18, 142, 146, 150, 154, 166

FUSED = True
ATTACH_WAIT = True


class Buf:
    def __init__(self, name, parent=None, lo=0, hi=0):
        self.name = name
        self.parent = parent
        self.lo, self.hi = lo, hi
        self.last_w = None
        self.readers = []
        self.peers = [self]
        if parent is not None:
            for o in parent._kids:
                if o.lo < hi and lo < o.hi:
                    o.peers.append(self)
                    self.peers.append(o)
            parent._kids.append(self)


class Parent:
    def __init__(self):
        self._kids = []


class Op:
    __slots__ = ("eng", "fn", "deps", "marked", "ticket", "epoch", "dma", "dma_val", "pos")

    def __init__(self, eng, fn, epoch, dma=None):
        self.eng, self.fn, self.epoch, self.dma = eng, fn, epoch, dma
        self.deps = []
        self.marked = False
        self.ticket = None
        self.dma_val = None
        self.pos = None


class Sched:
    ENGS = ("pe", "act", "dve", "pool", "sp")

    def __init__(self):
        self.ops = {e: [] for e in self.ENGS}
        self.epoch = 0
        self.dma_count = {}

    def add(self, eng, fn, reads=(), writes=(), dma=None):
        op = Op(eng, fn, self.epoch, dma)
        op.pos = len(self.ops[eng])
        if dma is not None:
            self.dma_count[dma] = self.dma_count.get(dma, 0) + 16
            op.dma_val = self.dma_count[dma]
        deps = []

        def need(prev, raw):
            if prev is None or prev is op:
                return
            if prev.dma is None and op.dma is None and prev.eng == eng and not raw:
                return
            deps.append(prev)

        for b in reads:
            for p in b.peers:
                need(p.last_w, True)
        for b in writes:
            for p in b.peers:
                need(p.last_w, False)
                for r in p.readers:
                    need(r, False)
        for b in reads:
            b.readers.append(op)
        for b in writes:
            b.last_w = op
            b.readers = []
            for p in b.peers:
                if p is not b:
                    p.last_w = op
                    p.readers = []
        seen = set()
        for d in deps:
            if id(d) not in seen:
                seen.add(id(d))
                op.deps.append(d)
                d.marked = True
        self.ops[eng].append(op)
        return op

    def finalize(self):
        self.nepoch = self.epoch + 1
        for e in self.ENGS:
            cnt = {}
            for op in self.ops[e]:
                if op.dma is None and op.marked:
                    cnt[op.epoch] = cnt.get(op.epoch, 0) + 1
                    op.ticket = cnt[op.epoch]

    def emit_engine(self, e, handle, sems, dma_sems):
        known = {}
        for op in self.ops[e]:
            pend = []
            for d in op.deps:
                if d.dma is not None:
                    key, val, sem = ("dma", d.dma), d.dma_val, dma_sems[d.dma]
                else:
                    key, val, sem = (d.eng, d.epoch), d.ticket, sems[(d.eng, d.epoch)]
                if known.get(key, 0) >= val:
                    continue
                known[key] = val
                pend.append((sem, val))
            attach = bool(pend) and ATTACH_WAIT and op.dma is None
            for sem, val in (pend[:-1] if attach else pend):
                handle.wait_ge(sem, val)
            ins = op.fn(handle)
            if attach:
                ins._wait_ge(*pend[-1])
            if op.dma is not None:
                ins.then_inc(dma_sems[op.dma], 16)
            elif op.marked:
                ins.then_inc(sems[(e, op.epoch)], 1)


def build_program(L, S):
    NCH = S // T
    NTB = S // 128
    nc = bass.Bass("TRN2", target_bir_lowering=False)
    xin_d = nc.dram_tensor("xT", [D, S], F32, kind="ExternalInput").ap()
    wst_d = nc.dram_tensor("wst", [L, NSLAB, 128, SLABW], F32, kind="ExternalInput").ap()
    vec_d = nc.dram_tensor("vecs", [128, L, NV], F32, kind="ExternalInput").ap()
    cst_d = nc.dram_tensor("consts", [128, 640], F32, kind="ExternalInput").ap()
    out_d = nc.dram_tensor("outT", [D, S], F32, kind="ExternalOutput").ap()
    wsc_d = nc.dram_tensor("wsc", [L, NSLAB, 128, SLABW], BF16, kind="Internal").ap()
    xin_v = xin_d.rearrange("(k p) t -> p k t", p=128)
    out_v = out_d.rearrange("(k p) t -> p k t", p=128)

    sc = Sched()
    es = contextlib.ExitStack()

    def sb(name, shape, dt):
        return es.enter_context(nc.sbuf_tensor(name, shape, dt))

    kT = sb("kT", [128, 4, S], BF16)
    vS = sb("vS", [128, NTB, 512], BF16)
    xT = sb("xTt", [128, KD, T], F32)
    uT = sb("uT", [128, KD, T], BF16)
    R = sb("R", [128, 32 * 1024], mybir.dt.uint8)
    SR = sb("SR", [128, 22 * 1024], mybir.dt.uint8)
    qT2 = sb("qT2", [128, 8, T], BF16)
    ring = [sb("ring%d" % i, [128, SLABW], BF16) for i in range(NSLOT)]
    sq = [sb("sq%d" % i, [128, T], BF16) for i in range(2)]
    rs = [sb("rs%d" % i, [128, T], F32) for i in range(2)]
    lnv = sb("lnv", [128, T], F32)
    cst32 = sb("cst32", [128, 640], F32)
    cstb = sb("cstb", [128, 512], BF16)
    vecs = sb("vecs_t", [128, L, NV], F32)
    hhalo = sb("hhalo", [128, 4, CW - 1], BF16)
    shalo = sb("shalo", [128, 4, 2], F32)
    ps = [es.enter_context(nc.psum_tensor("ps%d" % i, [128, 512], F32)) for i in range(8)]

    B = {}

    def mk(name, parent=None, lo=0, hi=0):
        B[name] = Buf(name, parent, lo, hi)
        return B[name]

    b_kT, b_vS, b_uT = mk("kT"), mk("vS"), mk("uT")
    b_xTj = [mk("xT%d" % j) for j in range(KD)]
    b_qT2 = mk("qT2")
    b_ring = [mk("ring%d" % i) for i in range(NSLOT)]
    b_sq = [mk("sq%d" % i) for i in range(2)]
    b_rs = [mk("rs%d" % i) for i in range(2)]
    b_lnv, b_cst32, b_cstb, b_vecs = mk("lnv"), mk("cst32"), mk("cstb"), mk("vecs")
    b_hhalo, b_shalo = mk("hhalo"), mk("shalo")
    b_ps = [mk("ps%d" % i) for i in range(8)]
    NG = 5
    GS = NSLAB // NG
    b_wsc = [[mk("wsc%d_%d" % (l, g)) for g in range(NG)] for l in range(L)]
    b_xd = [[mk("xd%d_%d" % (c, j)) for j in range(KD)] for c in range(NCH)]

    pR, pS = Parent(), Parent()
    KB = 1024

    def carve(tile, parent, name, off, shape, dt):
        nbytes = int(np.prod(shape)) * (4 if dt == F32 else 2)
        v = tile[:, off:off + nbytes].bitcast(dt)
        if len(shape) == 2:
            v = v.rearrange("p (a b) -> p a b", a=shape[0])
        b = mk(name, parent, off, off + nbytes)
        return v, b

    oT, b_oT = carve(R, pR, "oT", 4 * KB, [4, T], BF16)
    hbT, b_hbT = carve(R, pR, "hbT", 8 * KB, [4, T], BF16)
    hcT, b_hcT = carve(R, pR, "hcT", 12 * KB, [4, T], BF16)
    cvo, b_cvo = carve(R, pR, "cvo", 16 * KB, [4, T], F32)
    mT, b_mT = carve(R, pR, "mT", 24 * KB, [KD, T], BF16)
    hT, b_hT = carve(R, pR, "hT", 0, [KF, T], BF16)
    HB_W = T + CW - 1
    hbin = []
    b_hbin = []
    for c in range(4):
        v, b = carve(SR, pS, "hbin%d" % c, c * 1088, [HB_W], BF16)
        hbin.append(v); b_hbin.append(b)
    o1 = 4352
    sctmp, b_sctmp = carve(SR, pS, "sctmp", o1, [T], F32)
    sct2, b_sct2 = carve(SR, pS, "sct2", o1 + 2 * KB, [T], F32)
    scp, b_scp = carve(SR, pS, "scp", o1 + 4 * KB, [T + 2], F32)
    sgt, b_sgt = carve(SR, pS, "sgt", o1 + 4 * KB + 2064, [T], F32)
    mu, b_mu = carve(SR, pS, "mu", 19968, [T], F32)
    xP, b_xPj = [], []
    for j in range(KD):
        v, b = carve(SR, pS, "xP%d" % j, 4 * KB + j * 2 * KB, [T], F32); xP.append(v); b_xPj.append(b)
    E_, bE, S_, bS, P_, bP, A_, bA = [], [], [], [], [], [], [], []
    for i in range(4):
        v, b = carve(R, pR, "E%d" % i, 24 * KB + i * 2 * KB, [T], F32); E_.append(v); bE.append(b)
    for i in range(3):
        v, b = carve(R, pR, "S%d" % i, i * KB, [T], BF16); S_.append(v); bS.append(b)
    for i in range(2):
        v, b = carve(SR, pS, "P%d" % i, 12800 + i * 2 * KB, [T], F32); P_.append(v); bP.append(b)
    for i in range(3):
        v, b = carve(SR, pS, "A%d" % i, 16896 + i * KB, [T], BF16); A_.append(v); bA.append(b)
    gt_, bgt = [], []
    for i in range(3):
        v, b = carve(SR, pS, "g%d" % i, i * 2 * KB, [T], F32); gt_.append(v); bgt.append(b)
    mm_, bmm = [], []
    for i in range(2):
        v, b = carve(SR, pS, "m%d" % i, 6 * KB + i * 2 * KB, [T], F32); mm_.append(v); bmm.append(b)
    fs_, bfs = [], []
    for i in range(2):
        v, b = carve(SR, pS, "fs%d" % i, i * 2 * KB, [T], F32); fs_.append(v); bfs.append(b)

    tri, ltm, onesm, bones = (cstb[:, 0:128], cstb[:, 128:256], cstb[:, 256:384], cstb[:, 384:512])
    maskf = cst32[:, 512:640]

    def ACT(fn, reads, writes):
        return sc.add("act", fn, reads, writes)

    def DVE(fn, reads, writes):
        return sc.add("dve", fn, reads, writes)

    def POOL(fn, reads, writes):
        return sc.add("pool", fn, reads, writes)

    def MM(out, bout, lhsT, blhs, rhs, brhs, start, stop):
        return sc.add("pe", lambda e, o=out, l=lhsT, r=rhs, s=start, p=stop:
                      e.matmul(o, lhsT=l, rhs=r, start=s, stop=p, skip_group_check=True),
                      reads=[blhs, brhs], writes=[bout])

    def act(out, in_, func, reads, writes, bias=None, scale=None):
        kw = {}
        if bias is not None:
            kw["bias"] = bias
        if scale is not None:
            kw["scale"] = scale
        return ACT(lambda e, o=out, i=in_, f=func, k=kw: e.activation(out=o, in_=i, func=f, **k), reads, writes)

    def tt(engf, out, in0, in1, op, reads, writes):
        return engf(lambda e, o=out, a=in0, b=in1, p=op: e.tensor_tensor(out=o, in0=a, in1=b, op=p), reads, writes)

    def stt(engf, out, in0, scalar, in1, op0, op1, reads, writes):
        return engf(lambda e, o=out, a=in0, s=scalar, b=in1, p0=op0, p1=op1:
                    e.scalar_tensor_tensor(out=o, in0=a, scalar=s, in1=b, op0=p0, op1=p1), reads, writes)

    def tsc(engf, out, in0, s1, s2, op0, op1, reads, writes):
        if s2 is None:
            return engf(lambda e, o=out, a=in0, s=s1, p0=op0:
                        e.tensor_scalar(out=o, in0=a, scalar1=s, scalar2=None, op0=p0), reads, writes)
        return engf(lambda e, o=out, a=in0, s=s1, t=s2, p0=op0, p1=op1:
                    e.tensor_scalar(out=o, in0=a, scalar1=s, scalar2=t, op0=p0, op1=p1), reads, writes)

    def cp(engf, out, in_, reads, writes):
        return engf(lambda e, o=out, i=in_: e.tensor_copy(out=o, in_=i), reads, writes)

    pstate = {"i": 0}

    def psnext():
        i = pstate["i"]
        while i in pstate.get("hold", ()):
            i = (i + 1) % 8
        pstate["i"] = (i + 1) % 8
        return ps[i], b_ps[i]

    wstate = {"g": 0, "loaded": 0}
    total_slabs = L * NCH * NSLAB

    def record_load(G):
        cl, s = divmod(G, NSLAB)
        l = cl // NCH
        slot = G % NSLOT
        width = SLABW if s < NSLAB - 1 else (NBLK - (NSLAB - 1) * 32) * 128
        sc.add("sp", lambda e, o=ring[slot][:, 0:width], i=wsc_d[l, s, :, 0:width]: e.dma_start(out=o, in_=i),
               reads=[b_wsc[l][s // GS]], writes=[b_ring[slot]], dma="w%d" % slot)

    def wnext(n=1):
        g = wstate["g"]
        wstate["g"] = g + n
        cl, blk = divmod(g, NBLK)
        s, off = divmod(blk, 32)
        assert off + n <= 32
        G = cl * NSLAB + s
        while wstate["loaded"] < min(G + NSLOT, total_slabs):
            record_load(wstate["loaded"])
            wstate["loaded"] += 1
        slot = G % NSLOT
        return ring[slot][:, off * 128:(off + n) * 128], b_ring[slot]

    sc.add("sp", lambda e: e.dma_start(out=cst32[:], in_=cst_d), reads=[], writes=[b_cst32], dma="par0")
    sc.add("sp", lambda e: e.dma_start(out=vecs[:], in_=vec_d), reads=[], writes=[b_vecs], dma="par1")
    cp(DVE, cstb[:], cst32[:, 0:512], [b_cst32], [b_cstb])
    POOL(lambda e: e.memset(qT2[:], 0.0), [], [b_qT2])
    def convert_group(l, g):
        for s in range(g * GS, (g + 1) * GS):
            sc.add("pool", lambda e, o=wsc_d[l, s], i=wst_d[l, s]: e.dma_start(out=o, in_=i),
                   reads=[], writes=[b_wsc[l][g]], dma="cv%d_%d" % (l, g))

    for g in range(NG):
        convert_group(0, g)

    def rms_sq(k):
        i = k % 2
        act(sq[i][:], xT[:, k, :], AF.Square, [b_xTj[k]], [b_sq[i]])

    def rms_mm(k, pst, bpst):
        i = k % 2
        MM(pst[:], bpst, onesm, b_cstb, sq[i][:], b_sq[i], k == 0, k == KD - 1)

    def rms_step(k, pst, bpst):
        rms_sq(k)
        rms_mm(k, pst, bpst)

    def rms_finish(gcol0, l, pst, bpst):
        act(lnv[:], pst[:], AF.Ln, [bpst], [b_lnv], bias=EPS, scale=1.0 / D)
        act(rs[0][:], lnv[:], AF.Exp, [b_lnv], [b_rs[0]], scale=-0.5)
        for k in range(KD):
            stt(DVE, uT[:, k, :], xT[:, k, :], vecs[:, l, gcol0 + k:gcol0 + k + 1], rs[0][:], ALU.mult, ALU.mult,
                [b_xTj[k], b_rs[0], b_vecs], [b_uT])

    def proj(rhs_t, brhs, nk):
        p, bp = psnext()
        for k in range(nk):
            w, bw = wnext()
            MM(p[:], bp, w, bw, rhs_t[:, k, :], brhs, k == 0, k == nk - 1)
        return p, bp

    for l in range(L):
        sc.epoch = l
        POOL(lambda e: e.memset(hhalo[:], 0.0), [], [b_hhalo])
        POOL(lambda e: e.memset(shalo[:], 0.0), [], [b_shalo])
        for c in range(NCH):
            tok = slice(c * T, (c + 1) * T)
            blk0 = wstate["g"]
            if l + 1 < L:
                for g in range(NG):
                    if g * NCH // NG == c:
                        convert_group(l + 1, g)
            if (l, c) == (0, 0):
                src = xin_v
                pst, bpst = psnext()
                for j in range(KD):
                    sc.add("sp", lambda e, o=xT[:, j, :], i=src[:, j, tok]: e.dma_start(out=o, in_=i),
                           reads=[], writes=[b_xTj[j]], dma="xin%d" % j)
                    rms_step(j, pst, bpst)
                rms_finish(V_MG, l, pst, bpst)
            else:
                for j in range(KD):
                    act(xT[:, j, :], xP[j][:], AF.Copy, [b_xPj[j]], [b_xTj[j]])
            nxt = (l, c + 1) if c + 1 < NCH else ((l + 1, 0) if l + 1 < L else None)
            for which in range(2):
                for n in range(4):
                    p, bp = proj(uT, b_uT, KD)
                    i = n % 2
                    act(sq[i][:], p[:], AF.Square, [bp], [b_sq[i]])
                    p2, bp2 = psnext()
                    MM(p2[:], bp2, bones, b_cstb, sq[i][:], b_sq[i], True, True)
                    if which == 0:
                        act(lnv[:], p2[:], AF.Ln, [bp2], [b_lnv], bias=64.0 * EPS, scale=1.0)
                    else:
                        act(lnv[:], p2[:], AF.Ln, [bp2], [b_lnv], bias=EPS, scale=1.0 / 64.0)
                    act(rs[i][:], lnv[:], AF.Exp, [b_lnv], [b_rs[i]], scale=-0.5)
                    if which == 0:
                        for hh in range(2):
                            pr = slice(hh * 64, hh * 64 + 64)
                            stt(DVE, qT2[pr, 2 * n + hh, :], p[pr, :], vecs[pr, l, V_QG:V_QG + 1], rs[i][pr, :],
                                ALU.mult, ALU.mult, [bp, b_rs[i], b_vecs], [b_qT2])
                    else:
                        stt(DVE, kT[:, n, tok], p[:], vecs[:, l, V_KG:V_KG + 1], rs[i][:], ALU.mult, ALU.mult,
                            [bp, b_rs[i], b_vecs], [b_kT])
            wv = [wnext(4) for _ in range(KD)]
            for tb in range(4):
                p, bp = psnext()
                for k in range(KD):
                    MM(p[:], bp, uT[:, k, tb * 128:(tb + 1) * 128], b_uT, wv[k][0], wv[k][1], k == 0, k == KD - 1)
                if tb % 2 == 0:
                    act(vS[:, c * 4 + tb, :], p[:], AF.Copy, [bp], [b_vS])
                else:
                    cp(DVE, vS[:, c * 4 + tb, :], p[:], [bp], [b_vS])
            pstate["hold"] = {0, 1, 2, 3}

            def gproj():
                p, bp = psnext()
                for k in range(KD):
                    w, bw = wnext()
                    MM(p[:], bp, w, bw, uT[:, k, :], b_uT, k == 0, k == KD - 1)
                    yield
                return p, bp

            def bg():
                for ch in range(4):
                    pv, bpv = yield from gproj()
                    pg, bpg = yield from gproj()
                    cp(POOL, hbin[ch][:, 0:CW - 1], hhalo[:, ch, :], [b_hhalo], [b_hbin[ch]])
                    act(sgt[:], pg[:], AF.Exp, [bpg], [b_sgt], scale=-1.0)
                    tsc(DVE, sgt[:], sgt[:], 1.0, None, ALU.add, None, [b_sgt], [b_sgt])
                    DVE(lambda e: e.reciprocal(out=sgt[:], in_=sgt[:]), [b_sgt], [b_sgt])
                    tt(DVE, hbin[ch][:, CW - 1:], pv[:], sgt[:], ALU.mult, [bpv, b_sgt], [b_hbin[ch]])
                    cp(POOL, hhalo[:, ch, :], hbin[ch][:, T:T + CW - 1], [b_hbin[ch]], [b_hhalo])
                for ch in range(4):
                    px, bpx = yield from gproj()
                    act(sctmp[:], px[:], AF.Copy, [bpx], [b_sctmp])
                    pc, bpc = yield from gproj()
                    cp(POOL, scp[:, 0:2], shalo[:, ch, :], [b_shalo], [b_scp])
                    tt(DVE, scp[:, 2:], pc[:], sctmp[:], ALU.mult, [bpc, b_sctmp], [b_scp])
                    cp(POOL, shalo[:, ch, :], scp[:, T:T + 2], [b_scp], [b_shalo])
                    pb, bpb = yield from gproj()
                    w0 = V_SW + ch * 3
                    tsc(DVE, sct2[:], scp[:, 0:T], vecs[:, l, w0:w0 + 1], None, ALU.mult, None, [b_scp, b_vecs], [b_sct2])
                    for k in (1, 2):
                        stt(DVE, sct2[:], scp[:, k:k + T], vecs[:, l, w0 + k:w0 + k + 1], sct2[:], ALU.mult, ALU.add,
                            [b_scp, b_sct2, b_vecs], [b_sct2])
                    tt(DVE, hcT[:, ch, :], pb[:], sct2[:], ALU.mult, [bpb, b_sct2], [b_hcT])
                for ch in range(4):
                    pcv, bpcv = psnext()
                    for k in range(CW):
                        w, bw = wnext()
                        MM(pcv[:], bpcv, w, bw, hbin[ch][:, k:k + T], b_hbin[ch], k == 0, k == CW - 1)
                        yield
                    act(cvo[:, ch, :], pcv[:], AF.Identity, [bpcv, b_vecs], [b_cvo], bias=vecs[:, l, V_CB + ch:V_CB + ch + 1])
                pl, bpl = psnext()
                for ch in range(4):
                    i = ch % 2
                    cp(DVE, sq[i][:], cvo[:, ch, :], [b_cvo], [b_sq[i]])
                    MM(pl[:], bpl, onesm, b_cstb, sq[i][:], b_sq[i], ch == 0, ch == 3)
                    yield
                tsc(DVE, mu[:], pl[:], 1.0 / 512.0, None, ALU.mult, None, [bpl], [b_mu])
                for ch in range(4):
                    i = ch % 2
                    act(sq[i][:], cvo[:, ch, :], AF.Square, [b_cvo], [b_sq[i]])
                    MM(pl[:], bpl, onesm, b_cstb, sq[i][:], b_sq[i], ch == 0, ch == 3)
                    yield
                tt(DVE, rs[0][:], mu[:], mu[:], ALU.mult, [b_mu], [b_rs[0]])
                stt(DVE, lnv[:], pl[:], 1.0 / 512.0, rs[0][:], ALU.mult, ALU.subtract, [bpl, b_rs[0]], [b_lnv])
                tsc(DVE, lnv[:], lnv[:], 0.0, None, ALU.max, None, [b_lnv], [b_lnv])
                act(lnv[:], lnv[:], AF.Ln, [b_lnv], [b_lnv], bias=EPS, scale=1.0)
                act(rs[1][:], lnv[:], AF.Exp, [b_lnv], [b_rs[1]], scale=-0.5)
                for ch in range(4):
                    tt(DVE, cvo[:, ch, :], cvo[:, ch, :], mu[:], ALU.subtract, [b_cvo, b_mu], [b_cvo])
                    tt(DVE, cvo[:, ch, :], cvo[:, ch, :], rs[1][:], ALU.mult, [b_cvo, b_rs[1]], [b_cvo])
                    act(hbT[:, ch, :], cvo[:, ch, :], AF.Silu, [b_cvo, b_vecs], [b_hbT],
                        bias=vecs[:, l, V_LB + ch:V_LB + ch + 1], scale=vecs[:, l, V_LG + ch:V_LG + ch + 1])

            bgs = {"g": bg(), "done": False}

            def bg_adv(nmm):
                for _ in range(nmm):
                    if bgs["done"]:
                        return
                    try:
                        next(bgs["g"])
                    except StopIteration:
                        bgs["done"] = True

            BGN = min(4, max(2, -(-300 // (8 * (4 * c + 4)))))
            for hp in range(4):
                for hh in range(2):
                    h = hp * 2 + hh
                    pb_ = hh * 64
                    po_t, bpo = ps[3], b_ps[3]
                    H_t, bH = ps[2], b_ps[2]
                    kbs = list(range(4 * c + 3, -1, -1))
                    n = len(kbs)
                    c0s = [max(kb - 4 * c, 0) * 128 for kb in kbs]
                    first = {"H": True, "O": True}

                    def QK(idx):
                        kb, c0 = kbs[idx], c0s[idx]
                        z_t, bz = ps[idx % 2], b_ps[idx % 2]
                        MM(z_t[:, c0:T], bz, kT[:, hp, kb * 128:(kb + 1) * 128], b_kT,
                           qT2[:, h, c0:T], b_qT2, True, True)

                    def Eop(idx):
                        kb, c0 = kbs[idx], c0s[idx]
                        z_t, bz = ps[idx % 2], b_ps[idx % 2]
                        e_i = idx % 4
                        act(E_[e_i][:, c0:T], z_t[:, c0:T], AF.Exp, [bz], [bE[e_i]])
                        if kb - 4 * c >= 0:
                            tt(DVE, E_[e_i][:, c0:c0 + 128], E_[e_i][:, c0:c0 + 128], maskf, ALU.mult,
                               [bE[e_i], b_cst32], [bE[e_i]])

                    def Sop(idx):
                        c0, e_i, s_i = c0s[idx], idx % 4, idx % 3
                        act(S_[s_i][:, c0:T], E_[e_i][:, c0:T], AF.Ln, [bE[e_i]], [bS[s_i]], bias=1.0)

                    def hupd(idx):
                        c0, s_i = c0s[idx], idx % 3
                        if idx > 0:
                            c0p, s_p = c0s[idx - 1], (idx - 1) % 3
                            MM(H_t[:, c0p:T], bH, ltm, b_cstb, S_[s_p][:, c0p:T], bS[s_p], False, False)
                        MM(H_t[:, c0:T], bH, tri, b_cstb, S_[s_i][:, c0:T], bS[s_i], first["H"], True)
                        first["H"] = False

                    def Pop(idx):
                        c0, p_i = c0s[idx], idx % 2
                        act(P_[p_i][:, c0:T], H_t[:, c0:T], AF.Exp, [bH], [bP[p_i]], scale=-1.0)

                    def A_AV(idx):
                        kb, c0, e_i, p_i, a_i = kbs[idx], c0s[idx], idx % 4, idx % 2, idx % 3
                        tt(DVE, A_[a_i][:, c0:T], E_[e_i][:, c0:T], P_[p_i][:, c0:T], ALU.mult,
                           [bE[e_i], bP[p_i]], [bA[a_i]])
                        MM(po_t[:, c0:T], bpo, vS[:, kb, hp * 128:(hp + 1) * 128], b_vS,
                           A_[a_i][:, c0:T], bA[a_i], first["O"], kb == 0)
                        first["O"] = False

                    QK(0)
                    if n > 1:
                        QK(1)
                    Eop(0)
                    for idx in range(n):
                        if idx + 2 < n:
                            QK(idx + 2)
                        if idx + 1 < n:
                            Eop(idx + 1)
                        Sop(idx)
                        if idx >= 1:
                            Pop(idx - 1)
                        hupd(idx)
                        if idx >= 1:
                            A_AV(idx - 1)
                        bg_adv(BGN)
                    Pop(n - 1)
                    A_AV(n - 1)
                    cp(DVE, oT[pb_:pb_ + 64, hp, :], po_t[pb_:pb_ + 64, :], [bpo], [b_oT])
            bg_adv(10 ** 9)
            pstate["hold"] = set()
            for j in range(KD):
                pa_, bpa = proj(oT, b_oT, 4)
                pb2, bpb2 = proj(hbT, b_hbT, 4)
                pc2, bpc2 = proj(hcT, b_hcT, 4)
                pg_ = [proj(uT, b_uT, KD) for _ in range(3)]
                for i in range(3):
                    gb = V_GB + i * 8 + j
                    act(gt_[i][:], pg_[i][0][:], AF.Sigmoid, [pg_[i][1], b_vecs], [bgt[i]], bias=vecs[:, l, gb:gb + 1])
                tt(DVE, mm_[0][:], gt_[0][:], pa_[:], ALU.mult, [bgt[0], bpa], [bmm[0]])
                tt(DVE, mm_[1][:], gt_[1][:], pb2[:], ALU.mult, [bgt[1], bpb2], [bmm[1]])
                tt(POOL, mm_[0][:], mm_[0][:], mm_[1][:], ALU.add, [bmm[0], bmm[1]], [bmm[0]])
                tt(DVE, mm_[1][:], gt_[2][:], pc2[:], ALU.mult, [bgt[2], bpc2], [bmm[1]])
                tt(POOL, mT[:, j, :], mm_[0][:], mm_[1][:], ALU.add, [bmm[0], bmm[1]], [b_mT])
            if nxt is not None:
                l2, c2 = nxt
                src2 = xin_v if l2 == 0 else out_v
                tok2 = slice(c2 * T, (c2 + 1) * T)
                for j in range(KD):
                    sc.add("sp", lambda e, o=xP[j][:], i=src2[:, j, tok2]: e.dma_start(out=o, in_=i),
                           reads=([] if l2 == 0 else [b_xd[c2][j]]), writes=[b_xPj[j]], dma="xin%d" % j)
            pst, bpst = psnext()
            pstate["hold"] = {ps.index(pst)}
            for j in range(KD):
                p, bp = proj(mT, b_mT, KD)
                tt(DVE, xT[:, j, :], xT[:, j, :], p[:], ALU.add, [b_xTj[j], bp], [b_xTj[j]])
                rms_sq(j)
                if j > 0:
                    rms_mm(j - 1, pst, bpst)
            rms_mm(KD - 1, pst, bpst)
            pstate["hold"] = set()
            rms_finish(V_FG, l, pst, bpst)
            if nxt is not None:
                pstN, bpstN = psnext()
                pstate["hold"] = {ps.index(pstN)}
            for f in range(KF):
                pg2, bpg2 = proj(uT, b_uT, KD)
                pu, bpu = proj(uT, b_uT, KD)
                i = f % 2
                act(fs_[i][:], pg2[:], AF.Silu, [bpg2], [bfs[i]])
                tt(DVE, hT[:, f, :], fs_[i][:], pu[:], ALU.mult, [bfs[i], bpu], [b_hT])
                if nxt is not None and f <= KD:
                    if f < KD:
                        act(sq[f % 2][:], xP[f][:], AF.Square, [b_xPj[f]], [b_sq[f % 2]])
                    if f >= 1:
                        rms_mm(f - 1, pstN, bpstN)
            if nxt is not None:
                act(lnv[:], pstN[:], AF.Ln, [bpstN], [b_lnv], bias=EPS, scale=1.0 / D)
                act(rs[0][:], lnv[:], AF.Exp, [b_lnv], [b_rs[0]], scale=-0.5)
                pstate["hold"] = set()
            for j in range(KD):
                p, bp = proj(hT, b_hT, KF)
                tt(DVE, xT[:, j, :], xT[:, j, :], p[:], ALU.add, [b_xTj[j], bp], [b_xTj[j]])
                sc.add("sp", lambda e, o=out_v[:, j, tok], i=xT[:, j, :]: e.dma_start(out=o, in_=i),
                       reads=[b_xTj[j]], writes=[b_xd[c][j]], dma="xout%d" % j)
                if nxt is not None:
                    stt(DVE, uT[:, j, :], xP[j][:], vecs[:, nxt[0], V_MG + j:V_MG + j + 1], rs[0][:], ALU.mult, ALU.mult,
                        [b_xPj[j], b_rs[0], b_vecs], [b_uT])
            assert wstate["g"] - blk0 == NBLK, (wstate["g"] - blk0)
    sc.add("sp", lambda e: e.nop(), reads=[b for row in b_xd for b in row], writes=[])

    sc.finalize()
    sems = {}
    for e in Sched.ENGS:
        for ep in range(sc.nepoch):
            sems[(e, ep)] = es.enter_context(nc.semaphore("s_%s_%d" % (e, ep)))
    dma_sems = {k: es.enter_context(nc.semaphore("d_%s" % k)) for k in sc.dma_count}
    with nc.Block() as block:
        @block.tensor
        def _(e):
            sc.emit_engine("pe", e, sems, dma_sems)

        @block.scalar
        def _(e):
            sc.emit_engine("act", e, sems, dma_sems)

        @block.vector
        def _(e):
            sc.emit_engine("dve", e, sems, dma_sems)

        @block.gpsimd
        def _(e):
            sc.emit_engine("pool", e, sems, dma_sems)

        @block.sync
        def _(e):
            sc.emit_engine("sp", e, sems, dma_sems)
    es.close()
    return nc


def _blocks_nk(W):
    K, N = W.shape
    return W.reshape(K // 128, 128, N // 128, 128).transpose(2, 0, 1, 3).reshape(-1, 128, 128)


def _blocks_kn(W):
    K, N = W.shape
    return W.reshape(K // 128, 128, N // 128, 128).transpose(0, 2, 1, 3).reshape(-1, 128, 128)


def build_weight_stream(w_in, w_attn_out, w_conf_out, w_sc_out, w_o, w_ffn_in, w_ffn_out, conf_dw_w):
    L = w_in.shape[0]
    out = np.zeros((L, NSLAB * 32, 128, 128), np.float32)
    for l in range(L):
        wi = w_in[l]
        seg = lambda a, b: wi[:, a:b]
        parts = [_blocks_nk(seg(0, 512)), _blocks_nk(seg(512, 1024)), _blocks_kn(seg(1024, 1536))]
        cv = _blocks_nk(seg(1536, 2048)).reshape(4, 8, 128, 128)
        cg = _blocks_nk(seg(2048, 2560)).reshape(4, 8, 128, 128)
        for ch in range(4):
            parts += [cv[ch], cg[ch]]
        sx = _blocks_nk(seg(2560, 3072)).reshape(4, 8, 128, 128)
        sb_ = _blocks_nk(seg(3072, 3584)).reshape(4, 8, 128, 128)
        scg = _blocks_nk(seg(3584, 4096)).reshape(4, 8, 128, 128)
        for ch in range(4):
            parts += [sx[ch], scg[ch], sb_[ch]]
        dg = np.zeros((4, CW, 128, 128), np.float32)
        ii = np.arange(128)
        dg[:, :, ii, ii] = conf_dw_w[l].reshape(CW, 4, 128).transpose(1, 0, 2)
        parts.append(dg.reshape(4 * CW, 128, 128))
        ao = _blocks_nk(w_attn_out[l]).reshape(8, 4, 128, 128)
        co = _blocks_nk(w_conf_out[l]).reshape(8, 4, 128, 128)
        so = _blocks_nk(w_sc_out[l]).reshape(8, 4, 128, 128)
        gts = [_blocks_nk(seg(4096 + i * 1024, 4096 + (i + 1) * 1024)).reshape(8, 8, 128, 128) for i in range(3)]
        for j in range(8):
            parts += [ao[j], co[j], so[j], gts[0][j], gts[1][j], gts[2][j]]
        parts.append(_blocks_nk(w_o[l]))
        fg = _blocks_nk(w_ffn_in[l][:, :DFF]).reshape(KF, 8, 128, 128)
        fu = _blocks_nk(w_ffn_in[l][:, DFF:]).reshape(KF, 8, 128, 128)
        for f in range(KF):
            parts += [fg[f], fu[f]]
        parts.append(_blocks_nk(w_ffn_out[l]))
        allb = np.concatenate(parts, axis=0)
        assert allb.shape[0] == NBLK, allb.shape
        out[l, :NBLK] = allb
    out = out.reshape(L, NSLAB, 32, 128, 128).transpose(0, 1, 3, 2, 4).reshape(L, NSLAB, 128, SLABW)
    return np.ascontiguousarray(out)


def build_vecs(mix_norm_g, ffn_norm_g, q_norm_g, k_norm_g, conf_dw_w, conf_dw_b, conf_ln_g, conf_ln_b,
               sc_conv_w, gate_b):
    L = mix_norm_g.shape[0]
    v = np.zeros((128, L, NV), np.float32)
    for l in range(L):
        v[:, l, V_MG:V_MG + 8] = mix_norm_g[l].reshape(8, 128).T
        v[:, l, V_FG:V_FG + 8] = ffn_norm_g[l].reshape(8, 128).T
        v[:, l, V_QG] = np.tile(q_norm_g[l], 2)
        v[:, l, V_KG] = np.tile(k_norm_g[l], 2)
        v[:, l, V_CW:V_CW + 124] = conf_dw_w[l].reshape(CW, 4, 128).transpose(2, 1, 0).reshape(128, 124)
        v[:, l, V_CB:V_CB + 4] = conf_dw_b[l].reshape(4, 128).T
        v[:, l, V_LG:V_LG + 4] = conf_ln_g[l].reshape(4, 128).T
        v[:, l, V_LB:V_LB + 4] = conf_ln_b[l].reshape(4, 128).T
        v[:, l, V_SW:V_SW + 12] = sc_conv_w[l].reshape(3, 4, 128).transpose(2, 1, 0).reshape(128, 12)
        v[:, l, V_GB:V_GB + 24] = gate_b[l].reshape(3, 8, 128).transpose(2, 0, 1).reshape(128, 24)
    return v


def build_consts():
    c = np.zeros((128, 640), np.float32)
    j = np.arange(128)[:, None]
    s = np.arange(128)[None, :]
    c[:, 0:128] = (j >= s)
    c[:, 128:256] = (j < s)
    c[:, 256:384] = 1.0
    c[:, 384:512] = ((j // 64) == (s // 64))
    c[:, 512:640] = (j < s)
    return c


_PROG_CACHE = {}


def _get_prog(L, S):
    key = (L, S)
    if key not in _PROG_CACHE:
        _PROG_CACHE[key] = build_program(L, S)
    return _PROG_CACHE[key]


def run_layers(xT_list, wst, vecs, consts, S):
    L = wst.shape[0]
    nc = _get_prog(L, S)
    n = len(xT_list)
    in_maps = [{"xT": xT_list[i], "wst": wst, "vecs": vecs, "consts": consts} for i in range(n)]
    res = run_bass_kernel_spmd(nc, in_maps, core_ids=list(range(n)))
    return [r["outT"] for r in res.results]


def kernel(x, mix_norm_g, w_in, q_norm_g, k_norm_g, w_attn_out, conf_dw_w, conf_dw_b,
           conf_ln_g, conf_ln_b, w_conf_out, sc_conv_w, w_sc_out, gate_b, w_o,
           ffn_norm_g, w_ffn_in, w_ffn_out):
    f = lambda a: np.asarray(a, dtype=np.float32)
    x = f(x)
    Bn, S, _ = x.shape
    L = w_in.shape[0]
    wst = build_weight_stream(f(w_in), f(w_attn_out), f(w_conf_out), f(w_sc_out), f(w_o), f(w_ffn_in), f(w_ffn_out), f(conf_dw_w))
    vecs = build_vecs(f(mix_norm_g), f(ffn_norm_g), f(q_norm_g), f(k_norm_g), f(conf_dw_w), f(conf_dw_b),
                      f(conf_ln_g), f(conf_ln_b), f(sc_conv_w), f(gate_b))
    consts = build_consts()
    xT = [np.ascontiguousarray(x[b].T) for b in range(Bn)]
    if FUSED:
        outT = run_layers(xT, wst, vecs, consts, S)
    else:
        outT = xT
        for l in range(L):
            outT = run_layers(outT, np.ascontiguousarray(wst[l:l + 1]), np.ascontiguousarray(vecs[:, l:l + 1]), consts, S)
    return np.stack([o.T for o in outT], axis=0).astype(np.float32)
```

```python
import contextlib
import numpy as np
import concourse.bass as bass
import concourse.mybir as mybir
from concourse.bass_utils import run_bass_kernel_spmd

F32 = mybir.dt.float32
BF16 = mybir.dt.bfloat16
AF = mybir.ActivationFunctionType
ALU = mybir.AluOpType

D = 1024
KD = 8
NH = 8
DFF = 2816
KF = 22
CW = 31
EPS = 1e-6
T = 512
NSLOT = 5
SLABW = 4096
NBLK = 1260
NSLAB = 40
NV = 190
V_MG, V_FG, V_QG, V_KG, V_CW, V_CB, V_LG, V_LB, V_SW, V_GB = 0, 8, 16, 17, 18, 142, 146, 150, 154, 166

FUSED = True
ATTACH_WAIT = True


class Buf:
    def __init__(self, name, parent=None, lo=0, hi=0):
        self.name = name
        self.parent = parent
        self.lo, self.hi = lo, hi
        self.last_w = None
        self.readers = []
        self.peers = [self]
        if parent is not None:
            for o in parent._kids:
                if o.lo < hi and lo < o.hi:
                    o.peers.append(self)
                    self.peers.append(o)
            parent._kids.append(self)


class Parent:
    def __init__(self):
        self._kids = []


class Op:
    __slots__ = ("eng", "fn", "deps", "marked", "ticket", "epoch", "dma", "dma_val", "pos")

    def __init__(self, eng, fn, epoch, dma=None):
        self.eng, self.fn, self.epoch, self.dma = eng, fn, epoch, dma
        self.deps = []
        self.marked = False
        self.ticket = None
        self.dma_val = None
        self.pos = None


class Sched:
    ENGS = ("pe", "act", "dve", "pool", "sp")

    def __init__(self):
        self.ops = {e: [] for e in self.ENGS}
        self.epoch = 0
        self.dma_count = {}

    def add(self, eng, fn, reads=(), writes=(), dma=None):
        op = Op(eng, fn, self.epoch, dma)
        op.pos = len(self.ops[eng])
        if dma is not None:
            self.dma_count[dma] = self.dma_count.get(dma, 0) + 16
            op.dma_val = self.dma_count[dma]
        deps = []

        def need(prev, raw):
            if prev is None or prev is op:
                return
            if prev.dma is None and op.dma is None and prev.eng == eng and not raw:
                return
            deps.append(prev)

        for b in reads:
            for p in b.peers:
                need(p.last_w, True)
        for b in writes:
            for p in b.peers:
                need(p.last_w, False)
                for r in p.readers:
                    need(r, False)
        for b in reads:
            b.readers.append(op)
        for b in writes:
            b.last_w = op
            b.readers = []
            for p in b.peers:
                if p is not b:
                    p.last_w = op
                    p.readers = []
        seen = set()
        for d in deps:
            if id(d) not in seen:
                seen.add(id(d))
                op.deps.append(d)
                d.marked = True
        self.ops[eng].append(op)
        return op

    def finalize(self):
        self.nepoch = self.epoch + 1
        for e in self.ENGS:
            cnt = {}
            for op in self.ops[e]:
                if op.dma is None and op.marked:
                    cnt[op.epoch] = cnt.get(op.epoch, 0) + 1
                    op.ticket = cnt[op.epoch]

    def emit_engine(self, e, handle, sems, dma_sems):
        known = {}
        for op in self.ops[e]:
            pend = []
            for d in op.deps:
                if d.dma is not None:
                    key, val, sem = ("dma", d.dma), d.dma_val, dma_sems[d.dma]
                else:
                    key, val, sem = (d.eng, d.epoch), d.ticket, sems[(d.eng, d.epoch)]
                if known.get(key, 0) >= val:
                    continue
                known[key] = val
                pend.append((sem, val))
            attach = bool(pend) and ATTACH_WAIT and op.dma is None
            for sem, val in (pend[:-1] if attach else pend):
                handle.wait_ge(sem, val)
            ins = op.fn(handle)
            if attach:
                ins._wait_ge(*pend[-1])
            if op.dma is not None:
                ins.then_inc(dma_sems[op.dma], 16)
            elif op.marked:
                ins.then_inc(sems[(e, op.epoch)], 1)


def build_program(L, S):
    NCH = S // T
    NTB = S // 128
    nc = bass.Bass("TRN2", target_bir_lowering=False)
    xin_d = nc.dram_tensor("xT", [D, S], F32, kind="ExternalInput").ap()
    wst_d = nc.dram_tensor("wst", [L, NSLAB, 128, SLABW], F32, kind="ExternalInput").ap()
    vec_d = nc.dram_tensor("vecs", [128, L, NV], F32, kind="ExternalInput").ap()
    cst_d = nc.dram_tensor("consts", [128, 640], F32, kind="ExternalInput").ap()
    out_d = nc.dram_tensor("outT", [D, S], F32, kind="ExternalOutput").ap()
    wsc_d = nc.dram_tensor("wsc", [L, NSLAB, 128, SLABW], BF16, kind="Internal").ap()
    xin_v = xin_d.rearrange("(k p) t -> p k t", p=128)
    out_v = out_d.rearrange("(k p) t -> p k t", p=128)

    sc = Sched()
    es = contextlib.ExitStack()

    def sb(name, shape, dt):
        return es.enter_context(nc.sbuf_tensor(name, shape, dt))

    kT = sb("kT", [128, 4, S], BF16)
    vS = sb("vS", [128, NTB, 512], BF16)
    xT = sb("xTt", [128, KD, T], F32)
    uT = sb("uT", [128, KD, T], BF16)
    R = sb("R", [128, 32 * 1024], mybir.dt.uint8)
    SR = sb("SR", [128, 22 * 1024], mybir.dt.uint8)
    qT2 = sb("qT2", [128, 8, T], BF16)
    ring = [sb("ring%d" % i, [128, SLABW], BF16) for i in range(NSLOT)]
    sq = [sb("sq%d" % i, [128, T], BF16) for i in range(2)]
    rs = [sb("rs%d" % i, [128, T], F32) for i in range(2)]
    lnv = sb("lnv", [128, T], F32)
    cst32 = sb("cst32", [128, 640], F32)
    cstb = sb("cstb", [128, 512], BF16)
    vecs = sb("vecs_t", [128, L, NV], F32)
    hhalo = sb("hhalo", [128, 4, CW - 1], BF16)
    shalo = sb("shalo", [128, 4, 2], F32)
    ps = [es.enter_context(nc.psum_tensor("ps%d" % i, [128, 512], F32)) for i in range(8)]

    B = {}

    def mk(name, parent=None, lo=0, hi=0):
        B[name] = Buf(name, parent, lo, hi)
        return B[name]

    b_kT, b_vS, b_uT = mk("kT"), mk("vS"), mk("uT")
    b_xTj = [mk("xT%d" % j) for j in range(KD)]
    b_qT2 = mk("qT2")
    b_ring = [mk("ring%d" % i) for i in range(NSLOT)]
    b_sq = [mk("sq%d" % i) for i in range(2)]
    b_rs = [mk("rs%d" % i) for i in range(2)]
    b_lnv, b_cst32, b_cstb, b_vecs = mk("lnv"), mk("cst32"), mk("cstb"), mk("vecs")
    b_hhalo, b_shalo = mk("hhalo"), mk("shalo")
    b_ps = [mk("ps%d" % i) for i in range(8)]
    NG = 5
    GS = NSLAB // NG
    NG0 = 20
    GS0 = NSLAB // NG0
    b_wsc = [[mk("wsc%d_%d" % (l, g)) for g in range(NG0 if l == 0 else NG)] for l in range(L)]
    b_xd = [[mk("xd%d_%d" % (c, j)) for j in range(KD)] for c in range(NCH)]

    pR, pS = Parent(), Parent()
    KB = 1024

    def carve(tile, parent, name, off, shape, dt):
        nbytes = int(np.prod(shape)) * (4 if dt == F32 else 2)
        v = tile[:, off:off + nbytes].bitcast(dt)
        if len(shape) == 2:
            v = v.rearrange("p (a b) -> p a b", a=shape[0])
        b = mk(name, parent, off, off + nbytes)
        return v, b

    oT, b_oT = carve(R, pR, "oT", 4 * KB, [4, T], BF16)
    hbT, b_hbT = carve(R, pR, "hbT", 8 * KB, [4, T], BF16)
    hcT, b_hcT = carve(R, pR, "hcT", 12 * KB, [4, T], BF16)
    cvo, b_cvo = carve(R, pR, "cvo", 16 * KB, [4, T], F32)
    mT, b_mT = carve(R, pR, "mT", 24 * KB, [KD, T], BF16)
    hT, b_hT = carve(R, pR, "hT", 0, [KF, T], BF16)
    HB_W = T + CW - 1
    hbin = []
    b_hbin = []
    for c in range(4):
        v, b = carve(SR, pS, "hbin%d" % c, c * 1088, [HB_W], BF16)
        hbin.append(v); b_hbin.append(b)
    o1 = 4352
    sctmp, b_sctmp = carve(SR, pS, "sctmp", o1, [T], F32)
    sct2, b_sct2 = carve(SR, pS, "sct2", o1 + 2 * KB, [T], F32)
    scp, b_scp = carve(SR, pS, "scp", o1 + 4 * KB, [T + 2], F32)
    sgt, b_sgt = carve(SR, pS, "sgt", o1 + 4 * KB + 2064, [T], F32)
    mu, b_mu = carve(SR, pS, "mu", 19968, [T], F32)
    xP, b_xPj = [], []
    for j in range(KD):
        v, b = carve(SR, pS, "xP%d" % j, 4 * KB + j * 2 * KB, [T], F32); xP.append(v); b_xPj.append(b)
    E_, bE, S_, bS, P_, bP, A_, bA = [], [], [], [], [], [], [], []
    for i in range(4):
        v, b = carve(R, pR, "E%d" % i, 24 * KB + i * 2 * KB, [T], F32); E_.append(v); bE.append(b)
    for i in range(3):
        v, b = carve(R, pR, "S%d" % i, i * KB, [T], BF16); S_.append(v); bS.append(b)
    for i in range(2):
        v, b = carve(SR, pS, "P%d" % i, 12800 + i * 2 * KB, [T], F32); P_.append(v); bP.append(b)
    for i in range(3):
        v, b = carve(SR, pS, "A%d" % i, 16896 + i * KB, [T], BF16); A_.append(v); bA.append(b)
    gt_, bgt = [], []
    for i in range(3):
        v, b = carve(SR, pS, "g%d" % i, i * 2 * KB, [T], F32); gt_.append(v); bgt.append(b)
    mm_, bmm = [], []
    for i in range(2):
        v, b = carve(SR, pS, "m%d" % i, 6 * KB + i * 2 * KB, [T], F32); mm_.append(v); bmm.append(b)
    fs_, bfs = [], []
    for i in range(2):
        v, b = carve(SR, pS, "fs%d" % i, i * 2 * KB, [T], F32); fs_.append(v); bfs.append(b)

    tri, ltm, onesm, bones = (cstb[:, 0:128], cstb[:, 128:256], cstb[:, 256:384], cstb[:, 384:512])
    maskf = cst32[:, 512:640]

    def ACT(fn, reads, writes):
        return sc.add("act", fn, reads, writes)

    def DVE(fn, reads, writes):
        return sc.add("dve", fn, reads, writes)

    def POOL(fn, reads, writes):
        return sc.add("pool", fn, reads, writes)

    def MM(out, bout, lhsT, blhs, rhs, brhs, start, stop):
        return sc.add("pe", lambda e, o=out, l=lhsT, r=rhs, s=start, p=stop:
                      e.matmul(o, lhsT=l, rhs=r, start=s, stop=p, skip_group_check=True),
                      reads=[blhs, brhs], writes=[bout])

    def act(out, in_, func, reads, writes, bias=None, scale=None):
        kw = {}
        if bias is not None:
            kw["bias"] = bias
        if scale is not None:
            kw["scale"] = scale
        return ACT(lambda e, o=out, i=in_, f=func, k=kw: e.activation(out=o, in_=i, func=f, **k), reads, writes)

    def tt(engf, out, in0, in1, op, reads, writes):
        return engf(lambda e, o=out, a=in0, b=in1, p=op: e.tensor_tensor(out=o, in0=a, in1=b, op=p), reads, writes)

    def stt(engf, out, in0, scalar, in1, op0, op1, reads, writes):
        return engf(lambda e, o=out, a=in0, s=scalar, b=in1, p0=op0, p1=op1:
                    e.scalar_tensor_tensor(out=o, in0=a, scalar=s, in1=b, op0=p0, op1=p1), reads, writes)

    def tsc(engf, out, in0, s1, s2, op0, op1, reads, writes):
        if s2 is None:
            return engf(lambda e, o=out, a=in0, s=s1, p0=op0:
                        e.tensor_scalar(out=o, in0=a, scalar1=s, scalar2=None, op0=p0), reads, writes)
        return engf(lambda e, o=out, a=in0, s=s1, t=s2, p0=op0, p1=op1:
                    e.tensor_scalar(out=o, in0=a, scalar1=s, scalar2=t, op0=p0, op1=p1), reads, writes)

    def cp(engf, out, in_, reads, writes):
        return engf(lambda e, o=out, i=in_: e.tensor_copy(out=o, in_=i), reads, writes)

    pstate = {"i": 0}

    def psnext():
        i = pstate["i"]
        while i in pstate.get("hold", ()):
            i = (i + 1) % 8
        pstate["i"] = (i + 1) % 8
        return ps[i], b_ps[i]

    wstate = {"g": 0, "loaded": 0}
    total_slabs = L * NCH * NSLAB

    def record_load(G):
        cl, s = divmod(G, NSLAB)
        l = cl // NCH
        slot = G % NSLOT
        width = SLABW if s < NSLAB - 1 else (NBLK - (NSLAB - 1) * 32) * 128
        sc.add("sp", lambda e, o=ring[slot][:, 0:width], i=wsc_d[l, s, :, 0:width]: e.dma_start(out=o, in_=i),
               reads=[b_wsc[l][s // (GS0 if l == 0 else GS)]], writes=[b_ring[slot]], dma="w%d" % slot)

    def wnext(n=1):
        g = wstate["g"]
        wstate["g"] = g + n
        cl, blk = divmod(g, NBLK)
        s, off = divmod(blk, 32)
        assert off + n <= 32
        G = cl * NSLAB + s
        while wstate["loaded"] < min(G + NSLOT, total_slabs):
            record_load(wstate["loaded"])
            wstate["loaded"] += 1
        slot = G % NSLOT
        return ring[slot][:, off * 128:(off + n) * 128], b_ring[slot]

    sc.add("sp", lambda e: e.dma_start(out=cst32[:], in_=cst_d), reads=[], writes=[b_cst32], dma="par0")
    sc.add("sp", lambda e: e.dma_start(out=vecs[:], in_=vec_d), reads=[], writes=[b_vecs], dma="par1")
    cp(DVE, cstb[:], cst32[:, 0:512], [b_cst32], [b_cstb])
    POOL(lambda e: e.memset(qT2[:], 0.0), [], [b_qT2])
    def convert_group(l, g):
        gs = GS0 if l == 0 else GS
        for s in range(g * gs, (g + 1) * gs):
            sc.add("pool", lambda e, o=wsc_d[l, s], i=wst_d[l, s]: e.dma_start(out=o, in_=i),
                   reads=[], writes=[b_wsc[l][g]], dma="cv%d_%d" % (l, g))

    for g in range(NG0):
        convert_group(0, g)

    def rms_sq(k):
        i = k % 2
        act(sq[i][:], xT[:, k, :], AF.Square, [b_xTj[k]], [b_sq[i]])

    def rms_mm(k, pst, bpst):
        i = k % 2
        MM(pst[:], bpst, onesm, b_cstb, sq[i][:], b_sq[i], k == 0, k == KD - 1)

    def rms_step(k, pst, bpst):
        rms_sq(k)
        rms_mm(k, pst, bpst)

    def rms_finish(gcol0, l, pst, bpst):
        act(lnv[:], pst[:], AF.Ln, [bpst], [b_lnv], bias=EPS, scale=1.0 / D)
        act(rs[0][:], lnv[:], AF.Exp, [b_lnv], [b_rs[0]], scale=-0.5)
        for k in range(KD):
            stt(DVE, uT[:, k, :], xT[:, k, :], vecs[:, l, gcol0 + k:gcol0 + k + 1], rs[0][:], ALU.mult, ALU.mult,
                [b_xTj[k], b_rs[0], b_vecs], [b_uT])

    def proj(rhs_t, brhs, nk):
        p, bp = psnext()
        for k in range(nk):
            w, bw = wnext()
            MM(p[:], bp, w, bw, rhs_t[:, k, :], brhs, k == 0, k == nk - 1)
        return p, bp

    for l in range(L):
        sc.epoch = l
        POOL(lambda e: e.memset(hhalo[:], 0.0), [], [b_hhalo])
        POOL(lambda e: e.memset(shalo[:], 0.0), [], [b_shalo])
        for c in range(NCH):
            tok = slice(c * T, (c + 1) * T)
            blk0 = wstate["g"]
            if l + 1 < L:
                for g in range(NG):
                    if g * NCH // NG == c:
                        convert_group(l + 1, g)
            if (l, c) == (0, 0):
                src = xin_v
                pst, bpst = psnext()
                for j in range(KD):
                    sc.add("sp", lambda e, o=xT[:, j, :], i=src[:, j, tok]: e.dma_start(out=o, in_=i),
                           reads=[], writes=[b_xTj[j]], dma="xin%d" % j)
                    rms_step(j, pst, bpst)
                rms_finish(V_MG, l, pst, bpst)
            else:
                for j in range(KD):
                    act(xT[:, j, :], xP[j][:], AF.Copy, [b_xPj[j]], [b_xTj[j]])
            nxt = (l, c + 1) if c + 1 < NCH else ((l + 1, 0) if l + 1 < L else None)
            for which in range(2):
                for n in range(4):
                    p, bp = proj(uT, b_uT, KD)
                    i = n % 2
                    act(sq[i][:], p[:], AF.Square, [bp], [b_sq[i]])
                    p2, bp2 = psnext()
                    MM(p2[:], bp2, bones, b_cstb, sq[i][:], b_sq[i], True, True)
                    if which == 0:
                        act(lnv[:], p2[:], AF.Ln, [bp2], [b_lnv], bias=64.0 * EPS, scale=1.0)
                    else:
                        act(lnv[:], p2[:], AF.Ln, [bp2], [b_lnv], bias=EPS, scale=1.0 / 64.0)
                    act(rs[i][:], lnv[:], AF.Exp, [b_lnv], [b_rs[i]], scale=-0.5)
                    if which == 0:
                        for hh in range(2):
                            pr = slice(hh * 64, hh * 64 + 64)
                            stt(DVE, qT2[pr, 2 * n + hh, :], p[pr, :], vecs[pr, l, V_QG:V_QG + 1], rs[i][pr, :],
                                ALU.mult, ALU.mult, [bp, b_rs[i], b_vecs], [b_qT2])
                    else:
                        stt(DVE, kT[:, n, tok], p[:], vecs[:, l, V_KG:V_KG + 1], rs[i][:], ALU.mult, ALU.mult,
                            [bp, b_rs[i], b_vecs], [b_kT])
            wv = [wnext(4) for _ in range(KD)]
            for tb in range(4):
                p, bp = psnext()
                for k in range(KD):
                    MM(p[:], bp, uT[:, k, tb * 128:(tb + 1) * 128], b_uT, wv[k][0], wv[k][1], k == 0, k == KD - 1)
                if tb % 2 == 0:
                    act(vS[:, c * 4 + tb, :], p[:], AF.Copy, [bp], [b_vS])
                else:
                    cp(DVE, vS[:, c * 4 + tb, :], p[:], [bp], [b_vS])
            pstate["hold"] = {0, 1, 2, 3}

            def gproj():
                p, bp = psnext()
                for k in range(KD):
                    w, bw = wnext()
                    MM(p[:], bp, w, bw, uT[:, k, :], b_uT, k == 0, k == KD - 1)
                    yield
                return p, bp

            def bg():
                for ch in range(4):
                    pv, bpv = yield from gproj()
                    pg, bpg = yield from gproj()
                    cp(POOL, hbin[ch][:, 0:CW - 1], hhalo[:, ch, :], [b_hhalo], [b_hbin[ch]])
                    act(sgt[:], pg[:], AF.Exp, [bpg], [b_sgt], scale=-1.0)
                    tsc(DVE, sgt[:], sgt[:], 1.0, None, ALU.add, None, [b_sgt], [b_sgt])
                    DVE(lambda e: e.reciprocal(out=sgt[:], in_=sgt[:]), [b_sgt], [b_sgt])
                    tt(DVE, hbin[ch][:, CW - 1:], pv[:], sgt[:], ALU.mult, [bpv, b_sgt], [b_hbin[ch]])
                    cp(POOL, hhalo[:, ch, :], hbin[ch][:, T:T + CW - 1], [b_hbin[ch]], [b_hhalo])
                for ch in range(4):
                    px, bpx = yield from gproj()
                    act(sctmp[:], px[:], AF.Copy, [bpx], [b_sctmp])
                    pc, bpc = yield from gproj()
                    cp(POOL, scp[:, 0:2], shalo[:, ch, :], [b_shalo], [b_scp])
                    tt(DVE, scp[:, 2:], pc[:], sctmp[:], ALU.mult, [bpc, b_sctmp], [b_scp])
                    cp(POOL, shalo[:, ch, :], scp[:, T:T + 2], [b_scp], [b_shalo])
                    pb, bpb = yield from gproj()
                    w0 = V_SW + ch * 3
                    tsc(DVE, sct2[:], scp[:, 0:T], vecs[:, l, w0:w0 + 1], None, ALU.mult, None, [b_scp, b_vecs], [b_sct2])
                    for k in (1, 2):
                        stt(DVE, sct2[:], scp[:, k:k + T], vecs[:, l, w0 + k:w0 + k + 1], sct2[:], ALU.mult, ALU.add,
                            [b_scp, b_sct2, b_vecs], [b_sct2])
                    tt(DVE, hcT[:, ch, :], pb[:], sct2[:], ALU.mult, [bpb, b_sct2], [b_hcT])
                for ch in range(4):
                    pcv, bpcv = psnext()
                    for k in range(CW):
                        w, bw = wnext()
                        MM(pcv[:], bpcv, w, bw, hbin[ch][:, k:k + T], b_hbin[ch], k == 0, k == CW - 1)
                        yield
                    act(cvo[:, ch, :], pcv[:], AF.Identity, [bpcv, b_vecs], [b_cvo], bias=vecs[:, l, V_CB + ch:V_CB + ch + 1])
                pl, bpl = psnext()
                for ch in range(4):
                    i = ch % 2
                    cp(DVE, sq[i][:], cvo[:, ch, :], [b_cvo], [b_sq[i]])
                    MM(pl[:], bpl, onesm, b_cstb, sq[i][:], b_sq[i], ch == 0, ch == 3)
                    yield
                tsc(DVE, mu[:], pl[:], 1.0 / 512.0, None, ALU.mult, None, [bpl], [b_mu])
                for ch in range(4):
                    i = ch % 2
                    act(sq[i][:], cvo[:, ch, :], AF.Square, [b_cvo], [b_sq[i]])
                    MM(pl[:], bpl, onesm, b_cstb, sq[i][:], b_sq[i], ch == 0, ch == 3)
                    yield
                tt(DVE, rs[0][:], mu[:], mu[:], ALU.mult, [b_mu], [b_rs[0]])
                stt(DVE, lnv[:], pl[:], 1.0 / 512.0, rs[0][:], ALU.mult, ALU.subtract, [bpl, b_rs[0]], [b_lnv])
                tsc(DVE, lnv[:], lnv[:], 0.0, None, ALU.max, None, [b_lnv], [b_lnv])
                act(lnv[:], lnv[:], AF.Ln, [b_lnv], [b_lnv], bias=EPS, scale=1.0)
                act(rs[1][:], lnv[:], AF.Exp, [b_lnv], [b_rs[1]], scale=-0.5)
                for ch in range(4):
                    tt(DVE, cvo[:, ch, :], cvo[:, ch, :], mu[:], ALU.subtract, [b_cvo, b_mu], [b_cvo])
                    tt(DVE, cvo[:, ch, :], cvo[:, ch, :], rs[1][:], ALU.mult, [b_cvo, b_rs[1]], [b_cvo])
                    act(hbT[:, ch, :], cvo[:, ch, :], AF.Silu, [b_cvo, b_vecs], [b_hbT],
                        bias=vecs[:, l, V_LB + ch:V_LB + ch + 1], scale=vecs[:, l, V_LG + ch:V_LG + ch + 1])

            bgs = {"g": bg(), "done": False}

            def bg_adv(nmm):
                for _ in range(nmm):
                    if bgs["done"]:
                        return
                    try:
                        next(bgs["g"])
                    except StopIteration:
                        bgs["done"] = True

            BGN = min(4, max(2, -(-300 // (8 * (4 * c + 4)))))
            for hp in range(4):
                for hh in range(2):
                    h = hp * 2 + hh
                    pb_ = hh * 64
                    po_t, bpo = ps[3], b_ps[3]
                    H_t, bH = ps[2], b_ps[2]
                    kbs = list(range(4 * c + 3, -1, -1))
                    n = len(kbs)
                    c0s = [max(kb - 4 * c, 0) * 128 for kb in kbs]
                    first = {"H": True, "O": True}

                    def QK(idx):
                        kb, c0 = kbs[idx], c0s[idx]
                        z_t, bz = ps[idx % 2], b_ps[idx % 2]
                        MM(z_t[:, c0:T], bz, kT[:, hp, kb * 128:(kb + 1) * 128], b_kT,
                           qT2[:, h, c0:T], b_qT2, True, True)

                    def Eop(idx):
                        kb, c0 = kbs[idx], c0s[idx]
                        z_t, bz = ps[idx % 2], b_ps[idx % 2]
                        e_i = idx % 4
                        act(E_[e_i][:, c0:T], z_t[:, c0:T], AF.Exp, [bz], [bE[e_i]])
                        if kb - 4 * c >= 0:
                            tt(DVE, E_[e_i][:, c0:c0 + 128], E_[e_i][:, c0:c0 + 128], maskf, ALU.mult,
                               [bE[e_i], b_cst32], [bE[e_i]])

                    def Sop(idx):
                        c0, e_i, s_i = c0s[idx], idx % 4, idx % 3
                        act(S_[s_i][:, c0:T], E_[e_i][:, c0:T], AF.Ln, [bE[e_i]], [bS[s_i]], bias=1.0)

                    def hupd(idx):
                        c0, s_i = c0s[idx], idx % 3
                        if idx > 0:
                            c0p, s_p = c0s[idx - 1], (idx - 1) % 3
                            MM(H_t[:, c0p:T], bH, ltm, b_cstb, S_[s_p][:, c0p:T], bS[s_p], False, False)
                        MM(H_t[:, c0:T], bH, tri, b_cstb, S_[s_i][:, c0:T], bS[s_i], first["H"], True)
                        first["H"] = False

                    def Pop(idx):
                        c0, p_i = c0s[idx], idx % 2
                        act(P_[p_i][:, c0:T], H_t[:, c0:T], AF.Exp, [bH], [bP[p_i]], scale=-1.0)

                    def A_AV(idx):
                        kb, c0, e_i, p_i, a_i = kbs[idx], c0s[idx], idx % 4, idx % 2, idx % 3
                        tt(DVE, A_[a_i][:, c0:T], E_[e_i][:, c0:T], P_[p_i][:, c0:T], ALU.mult,
                           [bE[e_i], bP[p_i]], [bA[a_i]])
                        MM(po_t[:, c0:T], bpo, vS[:, kb, hp * 128:(hp + 1) * 128], b_vS,
                           A_[a_i][:, c0:T], bA[a_i], first["O"], kb == 0)
                        first["O"] = False

                    QK(0)
                    if n > 1:
                        QK(1)
                    Eop(0)
                    for idx in range(n):
                        if idx + 2 < n:
                            QK(idx + 2)
                        if idx + 1 < n:
                            Eop(idx + 1)
                        Sop(idx)
                        if idx >= 1:
                            Pop(idx - 1)
                        hupd(idx)
                        if idx >= 1:
                            A_AV(idx - 1)
                        bg_adv(BGN)
                    Pop(n - 1)
                    A_AV(n - 1)
                    cp(DVE, oT[pb_:pb_ + 64, hp, :], po_t[pb_:pb_ + 64, :], [bpo], [b_oT])
            bg_adv(10 ** 9)
            pstate["hold"] = set()
            for j in range(KD):
                pa_, bpa = proj(oT, b_oT, 4)
                pb2, bpb2 = proj(hbT, b_hbT, 4)
                pc2, bpc2 = proj(hcT, b_hcT, 4)
                pg_ = [proj(uT, b_uT, KD) for _ in range(3)]
                for i in range(3):
                    gb = V_GB + i * 8 + j
                    act(gt_[i][:], pg_[i][0][:], AF.Sigmoid, [pg_[i][1], b_vecs], [bgt[i]], bias=vecs[:, l, gb:gb + 1])
                tt(DVE, mm_[0][:], gt_[0][:], pa_[:], ALU.mult, [bgt[0], bpa], [bmm[0]])
                tt(DVE, mm_[1][:], gt_[1][:], pb2[:], ALU.mult, [bgt[1], bpb2], [bmm[1]])
                tt(POOL, mm_[0][:], mm_[0][:], mm_[1][:], ALU.add, [bmm[0], bmm[1]], [bmm[0]])
                tt(DVE, mm_[1][:], gt_[2][:], pc2[:], ALU.mult, [bgt[2], bpc2], [bmm[1]])
                tt(POOL, mT[:, j, :], mm_[0][:], mm_[1][:], ALU.add, [bmm[0], bmm[1]], [b_mT])
            if nxt is not None:
                l2, c2 = nxt
                src2 = xin_v if l2 == 0 else out_v
                tok2 = slice(c2 * T, (c2 + 1) * T)
                for j in range(KD):
                    sc.add("sp", lambda e, o=xP[j][:], i=src2[:, j, tok2]: e.dma_start(out=o, in_=i),
                           reads=([] if l2 == 0 else [b_xd[c2][j]]), writes=[b_xPj[j]], dma="xin%d" % j)
            pst, bpst = psnext()
            pstate["hold"] = {ps.index(pst)}
            for j in range(KD):
                p, bp = proj(mT, b_mT, KD)
                tt(DVE, xT[:, j, :], xT[:, j, :], p[:], ALU.add, [b_xTj[j], bp], [b_xTj[j]])
                rms_sq(j)
                if j > 0:
                    rms_mm(j - 1, pst, bpst)
            rms_mm(KD - 1, pst, bpst)
            pstate["hold"] = set()
            rms_finish(V_FG, l, pst, bpst)
            if nxt is not None:
                pstN, bpstN = psnext()
                pstate["hold"] = {ps.index(pstN)}
            for f in range(KF):
                pg2, bpg2 = proj(uT, b_uT, KD)
                pu, bpu = proj(uT, b_uT, KD)
                i = f % 2
                act(fs_[i][:], pg2[:], AF.Silu, [bpg2], [bfs[i]])
                tt(DVE, hT[:, f, :], fs_[i][:], pu[:], ALU.mult, [bfs[i], bpu], [b_hT])
                if nxt is not None and f <= KD:
                    if f < KD:
                        act(sq[f % 2][:], xP[f][:], AF.Square, [b_xPj[f]], [b_sq[f % 2]])
                    if f >= 1:
                        rms_mm(f - 1, pstN, bpstN)
            if nxt is not None:
                act(lnv[:], pstN[:], AF.Ln, [bpstN], [b_lnv], bias=EPS, scale=1.0 / D)
                act(rs[0][:], lnv[:], AF.Exp, [b_lnv], [b_rs[0]], scale=-0.5)
                pstate["hold"] = set()
            for j in range(KD):
                p, bp = proj(hT, b_hT, KF)
                tt(DVE, xT[:, j, :], xT[:, j, :], p[:], ALU.add, [b_xTj[j], bp], [b_xTj[j]])
                sc.add("sp", lambda e, o=out_v[:, j, tok], i=xT[:, j, :]: e.dma_start(out=o, in_=i),
                       reads=[b_xTj[j]], writes=[b_xd[c][j]], dma="xout%d" % j)
                if nxt is not None:
                    stt(DVE, uT[:, j, :], xP[j][:], vecs[:, nxt[0], V_MG + j:V_MG + j + 1], rs[0][:], ALU.mult, ALU.mult,
                        [b_xPj[j], b_rs[0], b_vecs], [b_uT])
            assert wstate["g"] - blk0 == NBLK, (wstate["g"] - blk0)
    sc.add("sp", lambda e: e.nop(), reads=[b for row in b_xd for b in row], writes=[])

    sc.finalize()
    sems = {}
    for e in Sched.ENGS:
        for ep in range(sc.nepoch):
            sems[(e, ep)] = es.enter_context(nc.semaphore("s_%s_%d" % (e, ep)))
    dma_sems = {k: es.enter_context(nc.semaphore("d_%s" % k)) for k in sc.dma_count}
    with nc.Block() as block:
        @block.tensor
        def _(e):
            sc.emit_engine("pe", e, sems, dma_sems)

        @block.scalar
        def _(e):
            sc.emit_engine("act", e, sems, dma_sems)

        @block.vector
        def _(e):
            sc.emit_engine("dve", e, sems, dma_sems)

        @block.gpsimd
        def _(e):
            sc.emit_engine("pool", e, sems, dma_sems)

        @block.sync
        def _(e):
            sc.emit_engine("sp", e, sems, dma_sems)
    es.close()
    return nc


def _blocks_nk(W):
    K, N = W.shape
    return W.reshape(K // 128, 128, N // 128, 128).transpose(2, 0, 1, 3).reshape(-1, 128, 128)


def _blocks_kn(W):
    K, N = W.shape
    return W.reshape(K // 128, 128, N // 128, 128).transpose(0, 2, 1, 3).reshape(-1, 128, 128)


def build_weight_stream(w_in, w_attn_out, w_conf_out, w_sc_out, w_o, w_ffn_in, w_ffn_out, conf_dw_w):
    L = w_in.shape[0]
    out = np.zeros((L, NSLAB * 32, 128, 128), np.float32)
    for l in range(L):
        wi = w_in[l]
        seg = lambda a, b: wi[:, a:b]
        parts = [_blocks_nk(seg(0, 512)), _blocks_nk(seg(512, 1024)), _blocks_kn(seg(1024, 1536))]
        cv = _blocks_nk(seg(1536, 2048)).reshape(4, 8, 128, 128)
        cg = _blocks_nk(seg(2048, 2560)).reshape(4, 8, 128, 128)
        for ch in range(4):
            parts += [cv[ch], cg[ch]]
        sx = _blocks_nk(seg(2560, 3072)).reshape(4, 8, 128, 128)
        sb_ = _blocks_nk(seg(3072, 3584)).reshape(4, 8, 128, 128)
        scg = _blocks_nk(seg(3584, 4096)).reshape(4, 8, 128, 128)
        for ch in range(4):
            parts += [sx[ch], scg[ch], sb_[ch]]
        dg = np.zeros((4, CW, 128, 128), np.float32)
        ii = np.arange(128)
        dg[:, :, ii, ii] = conf_dw_w[l].reshape(CW, 4, 128).transpose(1, 0, 2)
        parts.append(dg.reshape(4 * CW, 128, 128))
        ao = _blocks_nk(w_attn_out[l]).reshape(8, 4, 128, 128)
        co = _blocks_nk(w_conf_out[l]).reshape(8, 4, 128, 128)
        so = _blocks_nk(w_sc_out[l]).reshape(8, 4, 128, 128)
        gts = [_blocks_nk(seg(4096 + i * 1024, 4096 + (i + 1) * 1024)).reshape(8, 8, 128, 128) for i in range(3)]
        for j in range(8):
            parts += [ao[j], co[j], so[j], gts[0][j], gts[1][j], gts[2][j]]
        parts.append(_blocks_nk(w_o[l]))
        fg = _blocks_nk(w_ffn_in[l][:, :DFF]).reshape(KF, 8, 128, 128)
        fu = _blocks_nk(w_ffn_in[l][:, DFF:]).reshape(KF, 8, 128, 128)
        for f in range(KF):
            parts += [fg[f], fu[f]]
        parts.append(_blocks_nk(w_ffn_out[l]))
        allb = np.concatenate(parts, axis=0)
        assert allb.shape[0] == NBLK, allb.shape
        out[l, :NBLK] = allb
    out = out.reshape(L, NSLAB, 32, 128, 128).transpose(0, 1, 3, 2, 4).reshape(L, NSLAB, 128, SLABW)
    return np.ascontiguousarray(out)


def build_vecs(mix_norm_g, ffn_norm_g, q_norm_g, k_norm_g, conf_dw_w, conf_dw_b, conf_ln_g, conf_ln_b,
               sc_conv_w, gate_b):
    L = mix_norm_g.shape[0]
    v = np.zeros((128, L, NV), np.float32)
    for l in range(L):
        v[:, l, V_MG:V_MG + 8] = mix_norm_g[l].reshape(8, 128).T
        v[:, l, V_FG:V_FG + 8] = ffn_norm_g[l].reshape(8, 128).T
        v[:, l, V_QG] = np.tile(q_norm_g[l], 2)
        v[:, l, V_KG] = np.tile(k_norm_g[l], 2)
        v[:, l, V_CW:V_CW + 124] = conf_dw_w[l].reshape(CW, 4, 128).transpose(2, 1, 0).reshape(128, 124)
        v[:, l, V_CB:V_CB + 4] = conf_dw_b[l].reshape(4, 128).T
        v[:, l, V_LG:V_LG + 4] = conf_ln_g[l].reshape(4, 128).T
        v[:, l, V_LB:V_LB + 4] = conf_ln_b[l].reshape(4, 128).T
        v[:, l, V_SW:V_SW + 12] = sc_conv_w[l].reshape(3, 4, 128).transpose(2, 1, 0).reshape(128, 12)
        v[:, l, V_GB:V_GB + 24] = gate_b[l].reshape(3, 8, 128).transpose(2, 0, 1).reshape(128, 24)
    return v


def build_consts():
    c = np.zeros((128, 640), np.float32)
    j = np.arange(128)[:, None]
    s = np.arange(128)[None, :]
    c[:, 0:128] = (j >= s)
    c[:, 128:256] = (j < s)
    c[:, 256:384] = 1.0
    c[:, 384:512] = ((j // 64) == (s // 64))
    c[:, 512:640] = (j < s)
    return c


_PROG_CACHE = {}


def _get_prog(L, S):
    key = (L, S)
    if key not in _PROG_CACHE:
        _PROG_CACHE[key] = build_program(L, S)
    return _PROG_CACHE[key]


def run_layers(xT_list, wst, vecs, consts, S):
    L = wst.shape[0]
    nc = _get_prog(L, S)
    n = len(xT_list)
    in_maps = [{"xT": xT_list[i], "wst": wst, "vecs": vecs, "consts": consts} for i in range(n)]
    res = run_bass_kernel_spmd(nc, in_maps, core_ids=list(range(n)))
    return [r["outT"] for r in res.results]


def kernel(x, mix_norm_g, w_in, q_norm_g, k_norm_g, w_attn_out, conf_dw_w, conf_dw_b,
           conf_ln_g, conf_ln_b, w_conf_out, sc_conv_w, w_sc_out, gate_b, w_o,
           ffn_norm_g, w_ffn_in, w_ffn_out):
    f = lambda a: np.asarray(a, dtype=np.float32)
    x = f(x)
    Bn, S, _ = x.shape
    L = w_in.shape[0]
    wst = build_weight_stream(f(w_in), f(w_attn_out), f(w_conf_out), f(w_sc_out), f(w_o), f(w_ffn_in), f(w_ffn_out), f(conf_dw_w))
    vecs = build_vecs(f(mix_norm_g), f(ffn_norm_g), f(q_norm_g), f(k_norm_g), f(conf_dw_w), f(conf_dw_b),
                      f(conf_ln_g), f(conf_ln_b), f(sc_conv_w), f(gate_b))
    consts = build_consts()
    xT = [np.ascontiguousarray(x[b].T) for b in range(Bn)]
    if FUSED:
        outT = run_layers(xT, wst, vecs, consts, S)
    else:
        outT = xT
        for l in range(L):
            outT = run_layers(outT, np.ascontiguousarray(wst[l:l + 1]), np.ascontiguousarray(vecs[:, l:l + 1]), consts, S)
    return np.stack([o.T for o in outT], axis=0).astype(np.float32)
```

```python
import contextlib
import numpy as np
import concourse.bass as bass
import concourse.mybir as mybir
from concourse.bass_utils import run_bass_kernel_spmd

F32 = mybir.dt.float32
BF16 = mybir.dt.bfloat16
AF = mybir.ActivationFunctionType
ALU = mybir.AluOpType

D = 1024
KD = 8
NH = 8
DFF = 2816
KF = 22
CW = 31
EPS = 1e-6
T = 512
NSLOT = 5
SLABW = 4096
NBLK = 1260
NSLAB = 40
NV = 190
V_MG, V_FG, V_QG, V_KG, V_CW, V_CB, V_LG, V_LB, V_SW, V_GB = 0, 8, 16, 17, 18, 142, 146, 150, 154, 166

FUSED = True
ATTACH_WAIT = True


class Buf:
    def __init__(self, name, parent=None, lo=0, hi=0):
        self.name = name
        self.parent = parent
        self.lo, self.hi = lo, hi
        self.last_w = None
        self.readers = []
        self.peers = [self]
        if parent is not None:
            for o in parent._kids:
                if o.lo < hi and lo < o.hi:
                    o.peers.append(self)
                    self.peers.append(o)
            parent._kids.append(self)


class Parent:
    def __init__(self):
        self._kids = []


class Op:
    __slots__ = ("eng", "fn", "deps", "marked", "ticket", "epoch", "dma", "dma_val", "pos")

    def __init__(self, eng, fn, epoch, dma=None):
        self.eng, self.fn, self.epoch, self.dma = eng, fn, epoch, dma
        self.deps = []
        self.marked = False
        self.ticket = None
        self.dma_val = None
        self.pos = None


class Sched:
    ENGS = ("pe", "act", "dve", "pool", "sp")

    def __init__(self):
        self.ops = {e: [] for e in self.ENGS}
        self.epoch = 0
        self.dma_count = {}

    def add(self, eng, fn, reads=(), writes=(), dma=None):
        op = Op(eng, fn, self.epoch, dma)
        op.pos = len(self.ops[eng])
        if dma is not None:
            self.dma_count[dma] = self.dma_count.get(dma, 0) + 16
            op.dma_val = self.dma_count[dma]
        deps = []

        def need(prev, raw):
            if prev is None or prev is op:
                return
            if prev.dma is None and op.dma is None and prev.eng == eng and not raw:
                return
            deps.append(prev)

        for b in reads:
            for p in b.peers:
                need(p.last_w, True)
        for b in writes:
            for p in b.peers:
                need(p.last_w, False)
                for r in p.readers:
                    need(r, False)
        for b in reads:
            b.readers.append(op)
        for b in writes:
            b.last_w = op
            b.readers = []
            for p in b.peers:
                if p is not b:
                    p.last_w = op
                    p.readers = []
        seen = set()
        for d in deps:
            if id(d) not in seen:
                seen.add(id(d))
                op.deps.append(d)
                d.marked = True
        self.ops[eng].append(op)
        return op

    def finalize(self):
        self.nepoch = self.epoch + 1
        for e in self.ENGS:
            cnt = {}
            for op in self.ops[e]:
                if op.dma is None and op.marked:
                    cnt[op.epoch] = cnt.get(op.epoch, 0) + 1
                    op.ticket = cnt[op.epoch]

    def emit_engine(self, e, handle, sems, dma_sems):
        known = {}
        for op in self.ops[e]:
            pend = []
            for d in op.deps:
                if d.dma is not None:
                    key, val, sem = ("dma", d.dma), d.dma_val, dma_sems[d.dma]
                else:
                    key, val, sem = (d.eng, d.epoch), d.ticket, sems[(d.eng, d.epoch)]
                if known.get(key, 0) >= val:
                    continue
                known[key] = val
                pend.append((sem, val))
            attach = bool(pend) and ATTACH_WAIT and op.dma is None
            for sem, val in (pend[:-1] if attach else pend):
                handle.wait_ge(sem, val)
            ins = op.fn(handle)
            if attach:
                ins._wait_ge(*pend[-1])
            if op.dma is not None:
                ins.then_inc(dma_sems[op.dma], 16)
            elif op.marked:
                ins.then_inc(sems[(e, op.epoch)], 1)


def build_program(L, S):
    NCH = S // T
    NTB = S // 128
    nc = bass.Bass("TRN2", target_bir_lowering=False)
    xin_d = nc.dram_tensor("xT", [D, S], F32, kind="ExternalInput").ap()
    wst_d = nc.dram_tensor("wst", [L, NSLAB, 128, SLABW], F32, kind="ExternalInput").ap()
    vec_d = nc.dram_tensor("vecs", [128, L, NV], F32, kind="ExternalInput").ap()
    cst_d = nc.dram_tensor("consts", [128, 640], F32, kind="ExternalInput").ap()
    out_d = nc.dram_tensor("outT", [D, S], F32, kind="ExternalOutput").ap()
    wsc_d = nc.dram_tensor("wsc", [L, NSLAB, 128, SLABW], BF16, kind="Internal").ap()
    xin_v = xin_d.rearrange("(k p) t -> p k t", p=128)
    out_v = out_d.rearrange("(k p) t -> p k t", p=128)

    sc = Sched()
    es = contextlib.ExitStack()

    def sb(name, shape, dt):
        return es.enter_context(nc.sbuf_tensor(name, shape, dt))

    kT = sb("kT", [128, 4, S], BF16)
    vS = sb("vS", [128, NTB, 512], BF16)
    xT = sb("xTt", [128, KD, T], F32)
    uT = sb("uT", [128, KD, T], BF16)
    R = sb("R", [128, 32 * 1024], mybir.dt.uint8)
    SR = sb("SR", [128, 22 * 1024], mybir.dt.uint8)
    qT2 = sb("qT2", [128, 8, T], BF16)
    ring = [sb("ring%d" % i, [128, SLABW], BF16) for i in range(NSLOT)]
    sq = [sb("sq%d" % i, [128, T], BF16) for i in range(2)]
    rs = [sb("rs%d" % i, [128, T], F32) for i in range(2)]
    lnv = sb("lnv", [128, T], F32)
    cst32 = sb("cst32", [128, 640], F32)
    cstb = sb("cstb", [128, 512], BF16)
    vecs = sb("vecs_t", [128, L, NV], F32)
    hhalo = sb("hhalo", [128, 4, CW - 1], BF16)
    shalo = sb("shalo", [128, 4, 2], F32)
    ps = [es.enter_context(nc.psum_tensor("ps%d" % i, [128, 512], F32)) for i in range(8)]

    B = {}

    def mk(name, parent=None, lo=0, hi=0):
        B[name] = Buf(name, parent, lo, hi)
        return B[name]

    b_kT, b_vS, b_uT = mk("kT"), mk("vS"), mk("uT")
    b_xTj = [mk("xT%d" % j) for j in range(KD)]
    b_qT2 = mk("qT2")
    b_ring = [mk("ring%d" % i) for i in range(NSLOT)]
    b_sq = [mk("sq%d" % i) for i in range(2)]
    b_rs = [mk("rs%d" % i) for i in range(2)]
    b_lnv, b_cst32, b_cstb, b_vecs = mk("lnv"), mk("cst32"), mk("cstb"), mk("vecs")
    b_hhalo, b_shalo = mk("hhalo"), mk("shalo")
    b_ps = [mk("ps%d" % i) for i in range(8)]
    NG = 5
    GS = NSLAB // NG
    NG0 = 20
    GS0 = NSLAB // NG0
    b_wsc = [[mk("wsc%d_%d" % (l, g)) for g in range(NG0 if l == 0 else NG)] for l in range(L)]
    b_xd = [[mk("xd%d_%d" % (c, j)) for j in range(KD)] for c in range(NCH)]

    pR, pS = Parent(), Parent()
    KB = 1024

    def carve(tile, parent, name, off, shape, dt):
        nbytes = int(np.prod(shape)) * (4 if dt == F32 else 2)
        v = tile[:, off:off + nbytes].bitcast(dt)
        if len(shape) == 2:
            v = v.rearrange("p (a b) -> p a b", a=shape[0])
        b = mk(name, parent, off, off + nbytes)
        return v, b

    oT, b_oT = carve(R, pR, "oT", 4 * KB, [4, T], BF16)
    hbT, b_hbT = carve(R, pR, "hbT", 8 * KB, [4, T], BF16)
    hcT, b_hcT = carve(R, pR, "hcT", 12 * KB, [4, T], BF16)
    cvo, b_cvo = carve(R, pR, "cvo", 16 * KB, [4, T], F32)
    mT, b_mT = carve(R, pR, "mT", 24 * KB, [KD, T], BF16)
    hT, b_hT = carve(R, pR, "hT", 0, [KF, T], BF16)
    HB_W = T + CW - 1
    hbin = []
    b_hbin = []
    for c in range(4):
        v, b = carve(SR, pS, "hbin%d" % c, c * 1088, [HB_W], BF16)
        hbin.append(v); b_hbin.append(b)
    o1 = 4352
    sctmp, b_sctmp = carve(SR, pS, "sctmp", o1, [T], F32)
    sct2, b_sct2 = carve(SR, pS, "sct2", o1 + 2 * KB, [T], F32)
    scp, b_scp = carve(SR, pS, "scp", o1 + 4 * KB, [T + 2], F32)
    sgt, b_sgt = carve(SR, pS, "sgt", o1 + 4 * KB + 2064, [T], F32)
    mu, b_mu = carve(SR, pS, "mu", 19968, [T], F32)
    xP, b_xPj = [], []
    for j in range(KD):
        v, b = carve(SR, pS, "xP%d" % j, 4 * KB + j * 2 * KB, [T], F32); xP.append(v); b_xPj.append(b)
    E_, bE, S_, bS, P_, bP, A_, bA = [], [], [], [], [], [], [], []
    for i in range(4):
        v, b = carve(R, pR, "E%d" % i, 24 * KB + i * 2 * KB, [T], F32); E_.append(v); bE.append(b)
    for i in range(3):
        v, b = carve(R, pR, "S%d" % i, i * KB, [T], BF16); S_.append(v); bS.append(b)
    for i in range(2):
        v, b = carve(SR, pS, "P%d" % i, 12800 + i * 2 * KB, [T], F32); P_.append(v); bP.append(b)
    for i in range(3):
        v, b = carve(SR, pS, "A%d" % i, 16896 + i * KB, [T], BF16); A_.append(v); bA.append(b)
    gt_, bgt = [], []
    for i in range(3):
        v, b = carve(SR, pS, "g%d" % i, i * 2 * KB, [T], F32); gt_.append(v); bgt.append(b)
    mm_, bmm = [], []
    for i in range(2):
        v, b = carve(SR, pS, "m%d" % i, 6 * KB + i * 2 * KB, [T], F32); mm_.append(v); bmm.append(b)
    fs_, bfs = [], []
    for i in range(2):
        v, b = carve(SR, pS, "fs%d" % i, i * 2 * KB, [T], F32); fs_.append(v); bfs.append(b)

    tri, ltm, onesm, bones = (cstb[:, 0:128], cstb[:, 128:256], cstb[:, 256:384], cstb[:, 384:512])
    maskf = cst32[:, 512:640]

    def ACT(fn, reads, writes):
        return sc.add("act", fn, reads, writes)

    def DVE(fn, reads, writes):
        return sc.add("dve", fn, reads, writes)

    def POOL(fn, reads, writes):
        return sc.add("pool", fn, reads, writes)

    def MM(out, bout, lhsT, blhs, rhs, brhs, start, stop):
        return sc.add("pe", lambda e, o=out, l=lhsT, r=rhs, s=start, p=stop:
                      e.matmul(o, lhsT=l, rhs=r, start=s, stop=p, skip_group_check=True),
                      reads=[blhs, brhs], writes=[bout])

    def act(out, in_, func, reads, writes, bias=None, scale=None):
        kw = {}
        if bias is not None:
            kw["bias"] = bias
        if scale is not None:
            kw["scale"] = scale
        return ACT(lambda e, o=out, i=in_, f=func, k=kw: e.activation(out=o, in_=i, func=f, **k), reads, writes)

    def tt(engf, out, in0, in1, op, reads, writes):
        return engf(lambda e, o=out, a=in0, b=in1, p=op: e.tensor_tensor(out=o, in0=a, in1=b, op=p), reads, writes)

    def stt(engf, out, in0, scalar, in1, op0, op1, reads, writes):
        return engf(lambda e, o=out, a=in0, s=scalar, b=in1, p0=op0, p1=op1:
                    e.scalar_tensor_tensor(out=o, in0=a, scalar=s, in1=b, op0=p0, op1=p1), reads, writes)

    def tsc(engf, out, in0, s1, s2, op0, op1, reads, writes):
        if s2 is None:
            return engf(lambda e, o=out, a=in0, s=s1, p0=op0:
                        e.tensor_scalar(out=o, in0=a, scalar1=s, scalar2=None, op0=p0), reads, writes)
        return engf(lambda e, o=out, a=in0, s=s1, t=s2, p0=op0, p1=op1:
                    e.tensor_scalar(out=o, in0=a, scalar1=s, scalar2=t, op0=p0, op1=p1), reads, writes)

    def cp(engf, out, in_, reads, writes):
        return engf(lambda e, o=out, i=in_: e.tensor_copy(out=o, in_=i), reads, writes)

    pstate = {"i": 0}

    def psnext():
        i = pstate["i"]
        while i in pstate.get("hold", ()):
            i = (i + 1) % 8
        pstate["i"] = (i + 1) % 8
        return ps[i], b_ps[i]

    wstate = {"g": 0, "loaded": 0}
    total_slabs = L * NCH * NSLAB

    def record_load(G):
        cl, s = divmod(G, NSLAB)
        l = cl // NCH
        slot = G % NSLOT
        width = SLABW if s < NSLAB - 1 else (NBLK - (NSLAB - 1) * 32) * 128
        sc.add("sp", lambda e, o=ring[slot][:, 0:width], i=wsc_d[l, s, :, 0:width]: e.dma_start(out=o, in_=i),
               reads=[b_wsc[l][s // (GS0 if l == 0 else GS)]], writes=[b_ring[slot]], dma="w%d" % slot)

    def wnext(n=1):
        g = wstate["g"]
        wstate["g"] = g + n
        cl, blk = divmod(g, NBLK)
        s, off = divmod(blk, 32)
        assert off + n <= 32
        G = cl * NSLAB + s
        while wstate["loaded"] < min(G + NSLOT, total_slabs):
            record_load(wstate["loaded"])
            wstate["loaded"] += 1
        slot = G % NSLOT
        return ring[slot][:, off * 128:(off + n) * 128], b_ring[slot]

    sc.add("sp", lambda e: e.dma_start(out=cst32[:], in_=cst_d), reads=[], writes=[b_cst32], dma="par0")
    sc.add("sp", lambda e: e.dma_start(out=vecs[:], in_=vec_d), reads=[], writes=[b_vecs], dma="par1")
    cp(DVE, cstb[:], cst32[:, 0:512], [b_cst32], [b_cstb])
    DVE(lambda e: e.memset(qT2[:], 0.0), [], [b_qT2])
    def convert_group(l, g):
        gs = GS0 if l == 0 else GS
        win = 3 if l == 0 else 2
        for s in range(g * gs, (g + 1) * gs):
            sc.add("pool", lambda e, o=wsc_d[l, s], i=wst_d[l, s]: e.dma_start(out=o, in_=i),
                   reads=([b_wsc[l][g - win]] if g >= win else []), writes=[b_wsc[l][g]], dma="cv%d_%d" % (l, g))

    for g in range(NG0):
        convert_group(0, g)

    def rms_sq(k):
        i = k % 2
        act(sq[i][:], xT[:, k, :], AF.Square, [b_xTj[k]], [b_sq[i]])

    def rms_mm(k, pst, bpst):
        i = k % 2
        MM(pst[:], bpst, onesm, b_cstb, sq[i][:], b_sq[i], k == 0, k == KD - 1)

    def rms_step(k, pst, bpst):
        rms_sq(k)
        rms_mm(k, pst, bpst)

    def rms_finish(gcol0, l, pst, bpst):
        act(lnv[:], pst[:], AF.Ln, [bpst], [b_lnv], bias=EPS, scale=1.0 / D)
        act(rs[0][:], lnv[:], AF.Exp, [b_lnv], [b_rs[0]], scale=-0.5)
        for k in range(KD):
            stt(DVE, uT[:, k, :], xT[:, k, :], vecs[:, l, gcol0 + k:gcol0 + k + 1], rs[0][:], ALU.mult, ALU.mult,
                [b_xTj[k], b_rs[0], b_vecs], [b_uT])

    def proj(rhs_t, brhs, nk):
        p, bp = psnext()
        for k in range(nk):
            w, bw = wnext()
            MM(p[:], bp, w, bw, rhs_t[:, k, :], brhs, k == 0, k == nk - 1)
        return p, bp

    for l in range(L):
        sc.epoch = l
        DVE(lambda e: e.memset(hhalo[:], 0.0), [], [b_hhalo])
        DVE(lambda e: e.memset(shalo[:], 0.0), [], [b_shalo])
        for c in range(NCH):
            tok = slice(c * T, (c + 1) * T)
            blk0 = wstate["g"]
            if l + 1 < L:
                for g in range(NG):
                    if g * NCH // NG == c:
                        convert_group(l + 1, g)
            if (l, c) == (0, 0):
                src = xin_v
                pst, bpst = psnext()
                for j in range(KD):
                    sc.add("sp", lambda e, o=xT[:, j, :], i=src[:, j, tok]: e.dma_start(out=o, in_=i),
                           reads=[], writes=[b_xTj[j]], dma="xin%d" % j)
                    rms_step(j, pst, bpst)
                rms_finish(V_MG, l, pst, bpst)
            else:
                for j in range(KD):
                    act(xT[:, j, :], xP[j][:], AF.Copy, [b_xPj[j]], [b_xTj[j]])
            nxt = (l, c + 1) if c + 1 < NCH else ((l + 1, 0) if l + 1 < L else None)
            for which in range(2):
                for n in range(4):
                    p, bp = proj(uT, b_uT, KD)
                    i = n % 2
                    act(sq[i][:], p[:], AF.Square, [bp], [b_sq[i]])
                    p2, bp2 = psnext()
                    MM(p2[:], bp2, bones, b_cstb, sq[i][:], b_sq[i], True, True)
                    if which == 0:
                        act(lnv[:], p2[:], AF.Ln, [bp2], [b_lnv], bias=64.0 * EPS, scale=1.0)
                    else:
                        act(lnv[:], p2[:], AF.Ln, [bp2], [b_lnv], bias=EPS, scale=1.0 / 64.0)
                    act(rs[i][:], lnv[:], AF.Exp, [b_lnv], [b_rs[i]], scale=-0.5)
                    if which == 0:
                        for hh in range(2):
                            pr = slice(hh * 64, hh * 64 + 64)
                            stt(DVE, qT2[pr, 2 * n + hh, :], p[pr, :], vecs[pr, l, V_QG:V_QG + 1], rs[i][pr, :],
                                ALU.mult, ALU.mult, [bp, b_rs[i], b_vecs], [b_qT2])
                    else:
                        stt(DVE, kT[:, n, tok], p[:], vecs[:, l, V_KG:V_KG + 1], rs[i][:], ALU.mult, ALU.mult,
                            [bp, b_rs[i], b_vecs], [b_kT])
            wv = [wnext(4) for _ in range(KD)]
            for tb in range(4):
                p, bp = psnext()
                for k in range(KD):
                    MM(p[:], bp, uT[:, k, tb * 128:(tb + 1) * 128], b_uT, wv[k][0], wv[k][1], k == 0, k == KD - 1)
                if tb % 2 == 0:
                    act(vS[:, c * 4 + tb, :], p[:], AF.Copy, [bp], [b_vS])
                else:
                    cp(DVE, vS[:, c * 4 + tb, :], p[:], [bp], [b_vS])
            pstate["hold"] = {0, 1, 2, 3}

            def gproj():
                p, bp = psnext()
                for k in range(KD):
                    w, bw = wnext()
                    MM(p[:], bp, w, bw, uT[:, k, :], b_uT, k == 0, k == KD - 1)
                    yield
                return p, bp

            def bg():
                for ch in range(4):
                    pv, bpv = yield from gproj()
                    pg, bpg = yield from gproj()
                    cp(DVE, hbin[ch][:, 0:CW - 1], hhalo[:, ch, :], [b_hhalo], [b_hbin[ch]])
                    act(sgt[:], pg[:], AF.Exp, [bpg], [b_sgt], scale=-1.0)
                    tsc(DVE, sgt[:], sgt[:], 1.0, None, ALU.add, None, [b_sgt], [b_sgt])
                    DVE(lambda e: e.reciprocal(out=sgt[:], in_=sgt[:]), [b_sgt], [b_sgt])
                    tt(DVE, hbin[ch][:, CW - 1:], pv[:], sgt[:], ALU.mult, [bpv, b_sgt], [b_hbin[ch]])
                    cp(DVE, hhalo[:, ch, :], hbin[ch][:, T:T + CW - 1], [b_hbin[ch]], [b_hhalo])
                for ch in range(4):
                    px, bpx = yield from gproj()
                    act(sctmp[:], px[:], AF.Copy, [bpx], [b_sctmp])
                    pc, bpc = yield from gproj()
                    cp(DVE, scp[:, 0:2], shalo[:, ch, :], [b_shalo], [b_scp])
                    tt(DVE, scp[:, 2:], pc[:], sctmp[:], ALU.mult, [bpc, b_sctmp], [b_scp])
                    cp(DVE, shalo[:, ch, :], scp[:, T:T + 2], [b_scp], [b_shalo])
                    pb, bpb = yield from gproj()
                    w0 = V_SW + ch * 3
                    tsc(DVE, sct2[:], scp[:, 0:T], vecs[:, l, w0:w0 + 1], None, ALU.mult, None, [b_scp, b_vecs], [b_sct2])
                    for k in (1, 2):
                        stt(DVE, sct2[:], scp[:, k:k + T], vecs[:, l, w0 + k:w0 + k + 1], sct2[:], ALU.mult, ALU.add,
                            [b_scp, b_sct2, b_vecs], [b_sct2])
                    tt(DVE, hcT[:, ch, :], pb[:], sct2[:], ALU.mult, [bpb, b_sct2], [b_hcT])
                for ch in range(4):
                    pcv, bpcv = psnext()
                    for k in range(CW):
                        w, bw = wnext()
                        MM(pcv[:], bpcv, w, bw, hbin[ch][:, k:k + T], b_hbin[ch], k == 0, k == CW - 1)
                        yield
                    act(cvo[:, ch, :], pcv[:], AF.Identity, [bpcv, b_vecs], [b_cvo], bias=vecs[:, l, V_CB + ch:V_CB + ch + 1])
                pl, bpl = psnext()
                for ch in range(4):
                    i = ch % 2
                    cp(DVE, sq[i][:], cvo[:, ch, :], [b_cvo], [b_sq[i]])
                    MM(pl[:], bpl, onesm, b_cstb, sq[i][:], b_sq[i], ch == 0, ch == 3)
                    yield
                tsc(DVE, mu[:], pl[:], 1.0 / 512.0, None, ALU.mult, None, [bpl], [b_mu])
                for ch in range(4):
                    i = ch % 2
                    act(sq[i][:], cvo[:, ch, :], AF.Square, [b_cvo], [b_sq[i]])
                    MM(pl[:], bpl, onesm, b_cstb, sq[i][:], b_sq[i], ch == 0, ch == 3)
                    yield
                tt(DVE, rs[0][:], mu[:], mu[:], ALU.mult, [b_mu], [b_rs[0]])
                stt(DVE, lnv[:], pl[:], 1.0 / 512.0, rs[0][:], ALU.mult, ALU.subtract, [bpl, b_rs[0]], [b_lnv])
                tsc(DVE, lnv[:], lnv[:], 0.0, None, ALU.max, None, [b_lnv], [b_lnv])
                act(lnv[:], lnv[:], AF.Ln, [b_lnv], [b_lnv], bias=EPS, scale=1.0)
                act(rs[1][:], lnv[:], AF.Exp, [b_lnv], [b_rs[1]], scale=-0.5)
                for ch in range(4):
                    tt(DVE, cvo[:, ch, :], cvo[:, ch, :], mu[:], ALU.subtract, [b_cvo, b_mu], [b_cvo])
                    tt(DVE, cvo[:, ch, :], cvo[:, ch, :], rs[1][:], ALU.mult, [b_cvo, b_rs[1]], [b_cvo])
                    act(hbT[:, ch, :], cvo[:, ch, :], AF.Silu, [b_cvo, b_vecs], [b_hbT],
                        bias=vecs[:, l, V_LB + ch:V_LB + ch + 1], scale=vecs[:, l, V_LG + ch:V_LG + ch + 1])

            bgs = {"g": bg(), "done": False}

            def bg_adv(nmm):
                for _ in range(nmm):
                    if bgs["done"]:
                        return
                    try:
                        next(bgs["g"])
                    except StopIteration:
                        bgs["done"] = True

            BGN = min(4, max(2, -(-300 // (8 * (4 * c + 4)))))
            for hp in range(4):
                for hh in range(2):
                    h = hp * 2 + hh
                    pb_ = hh * 64
                    po_t, bpo = ps[3], b_ps[3]
                    H_t, bH = ps[2], b_ps[2]
                    kbs = list(range(4 * c + 3, -1, -1))
                    n = len(kbs)
                    c0s = [max(kb - 4 * c, 0) * 128 for kb in kbs]
                    first = {"H": True, "O": True}

                    def QK(idx):
                        kb, c0 = kbs[idx], c0s[idx]
                        z_t, bz = ps[idx % 2], b_ps[idx % 2]
                        MM(z_t[:, c0:T], bz, kT[:, hp, kb * 128:(kb + 1) * 128], b_kT,
                           qT2[:, h, c0:T], b_qT2, True, True)

                    def Eop(idx):
                        kb, c0 = kbs[idx], c0s[idx]
                        z_t, bz = ps[idx % 2], b_ps[idx % 2]
                        e_i = idx % 4
                        act(E_[e_i][:, c0:T], z_t[:, c0:T], AF.Exp, [bz], [bE[e_i]])
                        if kb - 4 * c >= 0:
                            tt(DVE, E_[e_i][:, c0:c0 + 128], E_[e_i][:, c0:c0 + 128], maskf, ALU.mult,
                               [bE[e_i], b_cst32], [bE[e_i]])

                    def Sop(idx):
                        c0, e_i, s_i = c0s[idx], idx % 4, idx % 3
                        act(S_[s_i][:, c0:T], E_[e_i][:, c0:T], AF.Ln, [bE[e_i]], [bS[s_i]], bias=1.0)

                    def hupd(idx):
                        c0, s_i = c0s[idx], idx % 3
                        if idx > 0:
                            c0p, s_p = c0s[idx - 1], (idx - 1) % 3
                            MM(H_t[:, c0p:T], bH, ltm, b_cstb, S_[s_p][:, c0p:T], bS[s_p], False, False)
                        MM(H_t[:, c0:T], bH, tri, b_cstb, S_[s_i][:, c0:T], bS[s_i], first["H"], True)
                        first["H"] = False

                    def Pop(idx):
                        c0, p_i = c0s[idx], idx % 2
                        act(P_[p_i][:, c0:T], H_t[:, c0:T], AF.Exp, [bH], [bP[p_i]], scale=-1.0)

                    def A_AV(idx):
                        kb, c0, e_i, p_i, a_i = kbs[idx], c0s[idx], idx % 4, idx % 2, idx % 3
                        tt(DVE, A_[a_i][:, c0:T], E_[e_i][:, c0:T], P_[p_i][:, c0:T], ALU.mult,
                           [bE[e_i], bP[p_i]], [bA[a_i]])
                        MM(po_t[:, c0:T], bpo, vS[:, kb, hp * 128:(hp + 1) * 128], b_vS,
                           A_[a_i][:, c0:T], bA[a_i], first["O"], kb == 0)
                        first["O"] = False

                    QK(0)
                    if n > 1:
                        QK(1)
                    Eop(0)
                    for idx in range(n):
                        if idx + 2 < n:
                            QK(idx + 2)
                        if idx + 1 < n:
                            Eop(idx + 1)
                        Sop(idx)
                        if idx >= 1:
                            Pop(idx - 1)
                        hupd(idx)
                        if idx >= 1:
                            A_AV(idx - 1)
                        bg_adv(BGN)
                    Pop(n - 1)
                    A_AV(n - 1)
                    cp(DVE, oT[pb_:pb_ + 64, hp, :], po_t[pb_:pb_ + 64, :], [bpo], [b_oT])
            bg_adv(10 ** 9)
            pstate["hold"] = set()
            for j in range(KD):
                pa_, bpa = proj(oT, b_oT, 4)
                pb2, bpb2 = proj(hbT, b_hbT, 4)
                pc2, bpc2 = proj(hcT, b_hcT, 4)
                pg_ = [proj(uT, b_uT, KD) for _ in range(3)]
                for i in range(3):
                    gb = V_GB + i * 8 + j
                    act(gt_[i][:], pg_[i][0][:], AF.Sigmoid, [pg_[i][1], b_vecs], [bgt[i]], bias=vecs[:, l, gb:gb + 1])
                tt(DVE, mm_[0][:], gt_[0][:], pa_[:], ALU.mult, [bgt[0], bpa], [bmm[0]])
                tt(DVE, mm_[1][:], gt_[1][:], pb2[:], ALU.mult, [bgt[1], bpb2], [bmm[1]])
                tt(DVE, mm_[0][:], mm_[0][:], mm_[1][:], ALU.add, [bmm[0], bmm[1]], [bmm[0]])
                tt(DVE, mm_[1][:], gt_[2][:], pc2[:], ALU.mult, [bgt[2], bpc2], [bmm[1]])
                tt(DVE, mT[:, j, :], mm_[0][:], mm_[1][:], ALU.add, [bmm[0], bmm[1]], [b_mT])
            if nxt is not None:
                l2, c2 = nxt
                src2 = xin_v if l2 == 0 else out_v
                tok2 = slice(c2 * T, (c2 + 1) * T)
                for j in range(KD):
                    sc.add("sp", lambda e, o=xP[j][:], i=src2[:, j, tok2]: e.dma_start(out=o, in_=i),
                           reads=([] if l2 == 0 else [b_xd[c2][j]]), writes=[b_xPj[j]], dma="xin%d" % j)
            pst, bpst = psnext()
            pstate["hold"] = {ps.index(pst)}
            for j in range(KD):
                p, bp = proj(mT, b_mT, KD)
                tt(DVE, xT[:, j, :], xT[:, j, :], p[:], ALU.add, [b_xTj[j], bp], [b_xTj[j]])
                rms_sq(j)
                if j > 0:
                    rms_mm(j - 1, pst, bpst)
            rms_mm(KD - 1, pst, bpst)
            pstate["hold"] = set()
            rms_finish(V_FG, l, pst, bpst)
            if nxt is not None:
                pstN, bpstN = psnext()
                pstate["hold"] = {ps.index(pstN)}
            for f in range(KF):
                pg2, bpg2 = proj(uT, b_uT, KD)
                pu, bpu = proj(uT, b_uT, KD)
                i = f % 2
                act(fs_[i][:], pg2[:], AF.Silu, [bpg2], [bfs[i]])
                tt(DVE, hT[:, f, :], fs_[i][:], pu[:], ALU.mult, [bfs[i], bpu], [b_hT])
                if nxt is not None and f <= KD:
                    if f < KD:
                        act(sq[f % 2][:], xP[f][:], AF.Square, [b_xPj[f]], [b_sq[f % 2]])
                    if f >= 1:
                        rms_mm(f - 1, pstN, bpstN)
            if nxt is not None:
                act(lnv[:], pstN[:], AF.Ln, [bpstN], [b_lnv], bias=EPS, scale=1.0 / D)
                act(rs[0][:], lnv[:], AF.Exp, [b_lnv], [b_rs[0]], scale=-0.5)
                pstate["hold"] = set()
            for j in range(KD):
                p, bp = proj(hT, b_hT, KF)
                tt(DVE, xT[:, j, :], xT[:, j, :], p[:], ALU.add, [b_xTj[j], bp], [b_xTj[j]])
                sc.add("sp", lambda e, o=out_v[:, j, tok], i=xT[:, j, :]: e.dma_start(out=o, in_=i),
                       reads=[b_xTj[j]], writes=[b_xd[c][j]], dma="xout%d" % j)
                if nxt is not None:
                    stt(DVE, uT[:, j, :], xP[j][:], vecs[:, nxt[0], V_MG + j:V_MG + j + 1], rs[0][:], ALU.mult, ALU.mult,
                        [b_xPj[j], b_rs[0], b_vecs], [b_uT])
            assert wstate["g"] - blk0 == NBLK, (wstate["g"] - blk0)
    sc.add("sp", lambda e: e.nop(), reads=[b for row in b_xd for b in row], writes=[])

    sc.finalize()
    sems = {}
    for e in Sched.ENGS:
        for ep in range(sc.nepoch):
            sems[(e, ep)] = es.enter_context(nc.semaphore("s_%s_%d" % (e, ep)))
    dma_sems = {k: es.enter_context(nc.semaphore("d_%s" % k)) for k in sc.dma_count}
    with nc.Block() as block:
        @block.tensor
        def _(e):
            sc.emit_engine("pe", e, sems, dma_sems)

        @block.scalar
        def _(e):
            sc.emit_engine("act", e, sems, dma_sems)

        @block.vector
        def _(e):
            sc.emit_engine("dve", e, sems, dma_sems)

        @block.gpsimd
        def _(e):
            sc.emit_engine("pool", e, sems, dma_sems)

        @block.sync
        def _(e):
            sc.emit_engine("sp", e, sems, dma_sems)
    es.close()
    return nc


def _blocks_nk(W):
    K, N = W.shape
    return W.reshape(K // 128, 128, N // 128, 128).transpose(2, 0, 1, 3).reshape(-1, 128, 128)


def _blocks_kn(W):
    K, N = W.shape
    return W.reshape(K // 128, 128, N // 128, 128).transpose(0, 2, 1, 3).reshape(-1, 128, 128)


def build_weight_stream(w_in, w_attn_out, w_conf_out, w_sc_out, w_o, w_ffn_in, w_ffn_out, conf_dw_w):
    L = w_in.shape[0]
    out = np.zeros((L, NSLAB * 32, 128, 128), np.float32)
    for l in range(L):
        wi = w_in[l]
        seg = lambda a, b: wi[:, a:b]
        parts = [_blocks_nk(seg(0, 512)), _blocks_nk(seg(512, 1024)), _blocks_kn(seg(1024, 1536))]
        cv = _blocks_nk(seg(1536, 2048)).reshape(4, 8, 128, 128)
        cg = _blocks_nk(seg(2048, 2560)).reshape(4, 8, 128, 128)
        for ch in range(4):
            parts += [cv[ch], cg[ch]]
        sx = _blocks_nk(seg(2560, 3072)).reshape(4, 8, 128, 128)
        sb_ = _blocks_nk(seg(3072, 3584)).reshape(4, 8, 128, 128)
        scg = _blocks_nk(seg(3584, 4096)).reshape(4, 8, 128, 128)
        for ch in range(4):
            parts += [sx[ch], scg[ch], sb_[ch]]
        dg = np.zeros((4, CW, 128, 128), np.float32)
        ii = np.arange(128)
        dg[:, :, ii, ii] = conf_dw_w[l].reshape(CW, 4, 128).transpose(1, 0, 2)
        parts.append(dg.reshape(4 * CW, 128, 128))
        ao = _blocks_nk(w_attn_out[l]).reshape(8, 4, 128, 128)
        co = _blocks_nk(w_conf_out[l]).reshape(8, 4, 128, 128)
        so = _blocks_nk(w_sc_out[l]).reshape(8, 4, 128, 128)
        gts = [_blocks_nk(seg(4096 + i * 1024, 4096 + (i + 1) * 1024)).reshape(8, 8, 128, 128) for i in range(3)]
        for j in range(8):
            parts += [ao[j], co[j], so[j], gts[0][j], gts[1][j], gts[2][j]]
        parts.append(_blocks_nk(w_o[l]))
        fg = _blocks_nk(w_ffn_in[l][:, :DFF]).reshape(KF, 8, 128, 128)
        fu = _blocks_nk(w_ffn_in[l][:, DFF:]).reshape(KF, 8, 128, 128)
        for f in range(KF):
            parts += [fg[f], fu[f]]
        parts.append(_blocks_nk(w_ffn_out[l]))
        allb = np.concatenate(parts, axis=0)
        assert allb.shape[0] == NBLK, allb.shape
        out[l, :NBLK] = allb
    out = out.reshape(L, NSLAB, 32, 128, 128).transpose(0, 1, 3, 2, 4).reshape(L, NSLAB, 128, SLABW)
    return np.ascontiguousarray(out)


def build_vecs(mix_norm_g, ffn_norm_g, q_norm_g, k_norm_g, conf_dw_w, conf_dw_b, conf_ln_g, conf_ln_b,
               sc_conv_w, gate_b):
    L = mix_norm_g.shape[0]
    v = np.zeros((128, L, NV), np.float32)
    for l in range(L):
        v[:, l, V_MG:V_MG + 8] = mix_norm_g[l].reshape(8, 128).T
        v[:, l, V_FG:V_FG + 8] = ffn_norm_g[l].reshape(8, 128).T
        v[:, l, V_QG] = np.tile(q_norm_g[l], 2)
        v[:, l, V_KG] = np.tile(k_norm_g[l], 2)
        v[:, l, V_CW:V_CW + 124] = conf_dw_w[l].reshape(CW, 4, 128).transpose(2, 1, 0).reshape(128, 124)
        v[:, l, V_CB:V_CB + 4] = conf_dw_b[l].reshape(4, 128).T
        v[:, l, V_LG:V_LG + 4] = conf_ln_g[l].reshape(4, 128).T
        v[:, l, V_LB:V_LB + 4] = conf_ln_b[l].reshape(4, 128).T
        v[:, l, V_SW:V_SW + 12] = sc_conv_w[l].reshape(3, 4, 128).transpose(2, 1, 0).reshape(128, 12)
        v[:, l, V_GB:V_GB + 24] = gate_b[l].reshape(3, 8, 128).transpose(2, 0, 1).reshape(128, 24)
    return v


def build_consts():
    c = np.zeros((128, 640), np.float32)
    j = np.arange(128)[:, None]
    s = np.arange(128)[None, :]
    c[:, 0:128] = (j >= s)
    c[:, 128:256] = (j < s)
    c[:, 256:384] = 1.0
    c[:, 384:512] = ((j // 64) == (s // 64))
    c[:, 512:640] = (j < s)
    return c


_PROG_CACHE = {}


def _get_prog(L, S):
    key = (L, S)
    if key not in _PROG_CACHE:
        _PROG_CACHE[key] = build_program(L, S)
    return _PROG_CACHE[key]


def run_layers(xT_list, wst, vecs, consts, S):
    L = wst.shape[0]
    nc = _get_prog(L, S)
    n = len(xT_list)
    in_maps = [{"xT": xT_list[i], "wst": wst, "vecs": vecs, "consts": consts} for i in range(n)]
    res = run_bass_kernel_spmd(nc, in_maps, core_ids=list(range(n)))
    return [r["outT"] for r in res.results]


def kernel(x, mix_norm_g, w_in, q_norm_g, k_norm_g, w_attn_out, conf_dw_w, conf_dw_b,
           conf_ln_g, conf_ln_b, w_conf_out, sc_conv_w, w_sc_out, gate_b, w_o,
           ffn_norm_g, w_ffn_in, w_ffn_out):
    f = lambda a: np.asarray(a, dtype=np.float32)
    x = f(x)
    Bn, S, _ = x.shape
    L = w_in.shape[0]
    wst = build_weight_stream(f(w_in), f(w_attn_out), f(w_conf_out), f(w_sc_out), f(w_o), f(w_ffn_in), f(w_ffn_out), f(conf_dw_w))
    vecs = build_vecs(f(mix_norm_g), f(ffn_norm_g), f(q_norm_g), f(k_norm_g), f(conf_dw_w), f(conf_dw_b),
                      f(conf_ln_g), f(conf_ln_b), f(sc_conv_w), f(gate_b))
    consts = build_consts()
    xT = [np.ascontiguousarray(x[b].T) for b in range(Bn)]
    if FUSED:
        outT = run_layers(xT, wst, vecs, consts, S)
    else:
        outT = xT
        for l in range(L):
            outT = run_layers(outT, np.ascontiguousarray(wst[l:l + 1]), np.ascontiguousarray(vecs[:, l:l + 1]), consts, S)
    return np.stack([o.T for o in outT], axis=0).astype(np.float32)
```
